# Optimizing a Trainium2 kernel written in Bass

```python
import math
import jax, jax.numpy as jnp
from jax import lax
import numpy as np

D_MODEL = 2048
BATCH = 16
SEQ = 256
DEPTH = 2
DEC_BATCH = 8
DEC_SEQ = 1024
PAST_LEN = 256

GRID_W = 64
DA_HEADS = 8
DA_HEAD_DIM = 64
DA_VDIM = 2 * DA_HEAD_DIM
DA_WIDTH = DA_HEADS * 2 * DA_HEAD_DIM
SC_WIDTH = 512
SC_K = 3
CF_WIDTH = 512
CF_K = 31
N_BRANCH = 3
FF_HIDDEN = -(-8 * D_MODEL // (3 * 256)) * 256
ROPE_THETA = 10000.0
Q_BLOCK = 128
EPS = 1e-6

OFF_K = DA_WIDTH
OFF_V = 2 * DA_WIDTH
OFF_SC = 3 * DA_WIDTH
OFF_CF = OFF_SC + 3 * SC_WIDTH
OFF_GATE = OFF_CF + 2 * CF_WIDTH
IN_COLS = OFF_GATE + N_BRANCH * D_MODEL

kernel_name = 'hybrid_diffattn_conv_prefix_dit_step'


def rmsnorm(x, g):
    xf = x.astype(jnp.float32)
    y = xf * lax.rsqrt(jnp.mean(xf * xf, axis=-1, keepdims=True) + EPS)
    return (y * g.astype(jnp.float32)).astype(x.dtype)


def layernorm(x, g, b):
    xf = x.astype(jnp.float32)
    mu = jnp.mean(xf, axis=-1, keepdims=True)
    xc = xf - mu
    y = xc * lax.rsqrt(jnp.mean(xc * xc, axis=-1, keepdims=True) + EPS)
    return (y * g.astype(jnp.float32) + b.astype(jnp.float32)).astype(x.dtype)


def dwconv(x, w, b=None):
    k = w.shape[0]
    y = lax.conv_general_dilated(x, w[:, None, :].astype(x.dtype), window_strides=(1,),
                                 padding=[(k // 2, k // 2)],
                                 dimension_numbers=('NWC', 'WIO', 'NWC'),
                                 feature_group_count=x.shape[-1])
    return y if b is None else y + b


def axial_angles(n_tok):
    rows = n_tok // GRID_W
    row_ids = jnp.repeat(jnp.arange(rows), GRID_W).astype(jnp.float32)
    col_ids = jnp.tile(jnp.arange(GRID_W), rows).astype(jnp.float32)
    n_freq = DA_HEAD_DIM // 4
    inv = ROPE_THETA ** (-jnp.arange(n_freq, dtype=jnp.float32) / n_freq)
    return row_ids[:, None] * inv, col_ids[:, None] * inv


def rope_half(x, ang):
    cos = jnp.cos(ang)[None, :, None, None, :].astype(x.dtype)
    sin = jnp.sin(ang)[None, :, None, None, :].astype(x.dtype)
    x1, x2 = jnp.split(x, 2, axis=-1)
    return jnp.concatenate([x1 * cos - x2 * sin, x2 * cos + x1 * sin], axis=-1)


def axial_rope(x, ang_row, ang_col):
    half = DA_HEAD_DIM // 2
    return jnp.concatenate([rope_half(x[..., :half], ang_row),
                            rope_half(x[..., half:], ang_col)], axis=-1)


def diff_attention(q, k, v, lam, lam_init, subln_g):
    b, lq = q.shape[0], q.shape[1]
    nb = lq // Q_BLOCK
    scale = DA_HEAD_DIM ** -0.5
    qb = q.reshape(b, nb, Q_BLOCK, DA_HEADS, 2, DA_HEAD_DIM).transpose(1, 0, 2, 3, 4, 5)

    def one_block(qblk):
        s = jnp.einsum('bqhcd,bkhcd->bchqk', qblk, k).astype(jnp.float32) * scale
        p = jax.nn.softmax(s, axis=-1)
        a = p[:, 0] - lam * p[:, 1]
        return jnp.einsum('bhqk,bkhe->bqhe', a.astype(v.dtype), v)

    o = lax.map(one_block, qb)
    o = o.transpose(1, 0, 2, 3, 4).reshape(b, lq, DA_HEADS, DA_VDIM)
    o = rmsnorm(o, subln_g) * (1.0 - lam_init)
    return o.reshape(b, lq, DA_HEADS * DA_VDIM)


def trunk_layer(x, mod, l, P, kv_ctx, angles):
    shift1, scale1, gate1, shift2, scale2, gate2 = jnp.split(mod, 6, axis=-1)
    b, n, _ = x.shape
    h = rmsnorm(x, P['g_norm1'][l]) * (1 + scale1) + shift1
    proj = h @ P['w_in'][l]
    q, k, v, sc, cf, gt = jnp.split(proj, [OFF_K, OFF_V, OFF_SC, OFF_CF, OFF_GATE], axis=-1)
    q = q.reshape(b, n, DA_HEADS, 2, DA_HEAD_DIM)
    k = k.reshape(b, n, DA_HEADS, 2, DA_HEAD_DIM)
    v = v.reshape(b, n, DA_HEADS, DA_VDIM)
    if angles is not None:
        q = axial_rope(q, *angles)
        k = axial_rope(k, *angles)
    if kv_ctx is None:
        k_all, v_all = k, v
    else:
        k_ctx, v_ctx = kv_ctx
        k_ctx = k_ctx.reshape(b, k_ctx.shape[1], DA_HEADS, 2, DA_HEAD_DIM).astype(k.dtype)
        k_all = jnp.concatenate([k_ctx, k], axis=1)
        v_all = jnp.concatenate([v_ctx.astype(v.dtype), v], axis=1)
    lam_init = 0.8 - 0.6 * math.exp(-0.3 * l)
    lq1, lk1, lq2, lk2 = [t.astype(jnp.float32) for t in P['da_lambda'][l]]
    lam = jnp.exp(jnp.sum(lq1 * lk1)) - jnp.exp(jnp.sum(lq2 * lk2)) + lam_init
    attn = diff_attention(q, k_all, v_all, lam, lam_init, P['da_subln'][l])
    branch_a = attn @ P['w_da_out'][l]
    g_b, g_c, sx = jnp.split(sc, 3, axis=-1)
    branch_b = (g_b * dwconv(g_c * sx, P['sc_conv'][l])) @ P['w_sc_out'][l]
    ca, cb = jnp.split(cf, 2, axis=-1)
    cy = dwconv(ca * jax.nn.sigmoid(cb), P['cf_conv'][l], P['cf_conv_b'][l])
    cy = jax.nn.silu(layernorm(cy, P['cf_ln_g'][l], P['cf_ln_b'][l]))
    branch_c = cy @ P['w_cf_out'][l]
    ga, gb, gc = jnp.split(jax.nn.sigmoid(gt + P['b_gate'][l]), 3, axis=-1)
    merged = ga * branch_a + gb * branch_b + gc * branch_c
    x = x + gate1 * (merged @ P['w_out'][l])
    h2 = rmsnorm(x, P['g_norm2'][l]) * (1 + scale2) + shift2
    u, w = jnp.split(h2 @ P['w_ffn_in'][l], 2, axis=-1)
    x = x + gate2 * ((jax.nn.silu(u) * w) @ P['w_ffn_out'][l])
    return x, (k.reshape(b, n, DA_HEADS, 2 * DA_HEAD_DIM), v)


def setup_inputs(seed: int = 0) -> dict:
    key = jax.random.key(seed)
    ks = jax.random.split(key, 32)
    f32 = jnp.float32

    def nrm(k, shape, scale=1.0):
        return jax.random.normal(k, shape, f32) * scale

    D = D_MODEL
    return {
        'x_prompt': nrm(ks[0], (BATCH, SEQ, D)),
        'x_sample': nrm(ks[1], (DEC_BATCH, DEC_SEQ, D)),
        'cache_k': nrm(ks[2], (DEC_BATCH, DEPTH, PAST_LEN, DA_HEADS, 2 * DA_HEAD_DIM)),
        'cache_v': nrm(ks[3], (DEC_BATCH, DEPTH, PAST_LEN, DA_HEADS, DA_VDIM)),
        'c': nrm(ks[4], (DEC_BATCH, D)),
        'c_ctx': nrm(ks[5], (D,)),
        'w_mod': nrm(ks[6], (DEPTH, D, 6 * D), 0.5 * D ** -0.5),
        'b_mod': nrm(ks[7], (DEPTH, 6 * D), 0.01),
        'g_norm1': 1.0 + nrm(ks[8], (DEPTH, D), 0.01),
        'w_in': nrm(ks[9], (DEPTH, D, IN_COLS), D ** -0.5),
        'da_lambda': nrm(ks[10], (DEPTH, 4, DA_HEAD_DIM), 0.1),
        'da_subln': 1.0 + nrm(ks[11], (DEPTH, DA_VDIM), 0.01),
        'w_da_out': nrm(ks[12], (DEPTH, DA_WIDTH, D), DA_WIDTH ** -0.5),
        'sc_conv': nrm(ks[13], (DEPTH, SC_K, SC_WIDTH), SC_K ** -0.5),
        'w_sc_out': nrm(ks[14], (DEPTH, SC_WIDTH, D), SC_WIDTH ** -0.5),
        'cf_conv': nrm(ks[15], (DEPTH, CF_K, CF_WIDTH), CF_K ** -0.5),
        'cf_conv_b': nrm(ks[16], (DEPTH, CF_WIDTH), 0.01),
        'cf_ln_g': 1.0 + nrm(ks[17], (DEPTH, CF_WIDTH), 0.01),
        'cf_ln_b': nrm(ks[18], (DEPTH, CF_WIDTH), 0.01),
        'w_cf_out': nrm(ks[19], (DEPTH, CF_WIDTH, D), CF_WIDTH ** -0.5),
        'b_gate': nrm(ks[20], (DEPTH, N_BRANCH * D), 0.01),
        'w_out': nrm(ks[21], (DEPTH, D, D), D ** -0.5),
        'g_norm2': 1.0 + nrm(ks[22], (DEPTH, D), 0.01),
        'w_ffn_in': nrm(ks[23], (DEPTH, D, 2 * FF_HIDDEN), D ** -0.5),
        'w_ffn_out': nrm(ks[24], (DEPTH, FF_HIDDEN, D), FF_HIDDEN ** -0.5),
        'g_final': 1.0 + nrm(ks[25], (D,), 0.01),
    }


def reference(x_prompt, x_sample, cache_k, cache_v, c, c_ctx, w_mod, b_mod, g_norm1, w_in,
              da_lambda, da_subln, w_da_out, sc_conv, w_sc_out, cf_conv, cf_conv_b, cf_ln_g,
              cf_ln_b, w_cf_out, b_gate, w_out, g_norm2, w_ffn_in, w_ffn_out, g_final):
    P = dict(w_mod=w_mod, b_mod=b_mod, g_norm1=g_norm1, w_in=w_in, da_lambda=da_lambda,
             da_subln=da_subln, w_da_out=w_da_out, sc_conv=sc_conv, w_sc_out=w_sc_out,
             cf_conv=cf_conv, cf_conv_b=cf_conv_b, cf_ln_g=cf_ln_g, cf_ln_b=cf_ln_b,
             w_cf_out=w_cf_out, b_gate=b_gate, w_out=w_out, g_norm2=g_norm2,
             w_ffn_in=w_ffn_in, w_ffn_out=w_ffn_out)

    xp = x_prompt
    ks_new, vs_new = [], []
    for l in range(DEPTH):
        mod_ctx = jax.nn.silu(c_ctx) @ w_mod[l] + b_mod[l]
        xp, (k_l, v_l) = trunk_layer(xp, mod_ctx, l, P, None, None)
        ks_new.append(k_l)
        vs_new.append(v_l)
    y_prompt = rmsnorm(xp, g_final)
    new_k = jnp.stack(ks_new, axis=1)
    new_v = jnp.stack(vs_new, axis=1)

    xs = x_sample
    angles = axial_angles(xs.shape[1])
    for l in range(DEPTH):
        mod_lat = (jax.nn.silu(c) @ w_mod[l] + b_mod[l])[:, None, :]
        xs, _ = trunk_layer(xs, mod_lat, l, P, (cache_k[:, l], cache_v[:, l]), angles)
    y_sample = rmsnorm(xs, g_final)

    return (y_prompt, y_sample, new_k, new_v)
```

```python
import contextlib
import math
import numpy as np
import concourse.bass as bass
import concourse.mybir as mybir
from concourse.bass_utils import run_bass_kernel_spmd

F32 = mybir.dt.float32
BF16 = mybir.dt.bfloat16
AF = mybir.ActivationFunctionType
ALU = mybir.AluOpType

D = 2048
NTOK = 1536
DEPTH = 2
FF = 5632
EPS = 1e-6
OFF_K, OFF_V, OFF_SC, OFF_CF, OFF_GATE = 1024, 2048, 3072, 4608, 5632
IN_COLS = 11776
ENGS = ("pe", "act", "dve", "pool", "sp")


class Prog:
    def __init__(self, nc):
        self.nc = nc
        self.stack = contextlib.ExitStack()
        self.ops = {e: [] for e in ENGS}
        self.esem = {}
        self.ecnt = {e: 0 for e in ENGS}
        for e in ("pe", "act", "dve", "pool"):
            self.esem[e] = self.stack.enter_context(nc.semaphore("es_" + e))
        self.dsems = {}
        self.waited = {e: {} for e in ENGS}
        self.last_w = {}
        self.readers = {}
        self.semobj = {}

    def sbuf(self, name, shape, dtype):
        return self.stack.enter_context(self.nc.sbuf_tensor(name, list(shape), dtype))

    def psum(self, name, shape, dtype):
        return self.stack.enter_context(self.nc.psum_tensor(name, list(shape), dtype))

    def dsem(self, name):
        s = self.stack.enter_context(self.nc.semaphore(name))
        self.dsems[name] = [s, 0, None]
        return name

    def _deps(self, r, w):
        evs = []
        for k in r:
            evs.append(self.last_w.get(k))
        for k in w:
            evs.append(self.last_w.get(k))
            evs.extend(self.readers.get(k, {}).values())
        return evs

    def _commit(self, ev, r, w):
        for k in r:
            d = self.readers.setdefault(k, {})
            old = d.get(id(ev[0]))
            if old is None or old[1] < ev[1]:
                d[id(ev[0])] = ev
        for k in w:
            self.last_w[k] = ev
            self.readers[k] = {}

    def _filter(self, eng, evs):
        ws = []
        seen = self.waited[eng]
        for ev in evs:
            if ev is None:
                continue
            sem, val = ev[0], ev[1]
            k = id(sem)
            if seen.get(k, 0) >= val:
                continue
            seen[k] = val
            self.semobj[k] = sem
            ws.append((sem, val))
        return ws

    def op(self, eng, fn, r=(), w=(), extra=()):
        ws = self._filter(eng, self._deps(r, w) + list(extra))
        self.ecnt[eng] += 1
        ev = (self.esem[eng], self.ecnt[eng])
        self.ops[eng].append((fn, ws, (self.esem[eng], 1)))
        self._commit(ev, r, w)
        return ev

    def mm(self, fns, r=(), w=()):
        ws = self._filter("pe", self._deps(r, w))
        n = len(fns)
        ev = None
        for i, fn in enumerate(fns):
            inc = None
            if i == n - 1:
                self.ecnt["pe"] += 1
                ev = (self.esem["pe"], self.ecnt["pe"])
                inc = (self.esem["pe"], 1)
            self.ops["pe"].append((fn, ws if i == 0 else [], inc))
        self._commit(ev, r, w)
        return ev

    def dma(self, queue, pairs, sem, r=(), w=(), **kw):
        rec = self.dsems[sem]
        evs = self._deps(r, w)
        if rec[2] is not None:
            evs.append(rec[2])
        ws = self._filter(queue, evs)
        for i, (o, i_) in enumerate(pairs):
            rec[1] += 16
            self.ops[queue].append(
                (lambda e, o=o, i_=i_: e.dma_start(out=o, in_=i_, **kw), ws if i == 0 else [], (rec[0], 16)))
        ev = (rec[0], rec[1])
        rec[2] = ev
        self._commit(ev, r, w)
        return ev

    def barrier(self, full=False):
        evs = [(self.esem[e], self.ecnt[e]) for e in ("pe", "act", "dve", "pool") if self.ecnt[e] > 0]
        for name, rec in self.dsems.items():
            if rec[2] is not None and (full or not name.startswith("s_w")):
                evs.append(rec[2])
        for e in ENGS:
            ws = self._filter(e, evs)
            if ws:
                self.ops[e].append((lambda en: en.nop(), ws, None))
        keep_w = {k: v for k, v in self.last_w.items() if isinstance(k, tuple) and k[0] == "ring"}
        keep_r = {k: v for k, v in self.readers.items() if isinstance(k, tuple) and k[0] == "ring"}
        self.last_w = {} if full else keep_w
        self.readers = {} if full else keep_r

    def emit(self):
        nc = self.nc
        with nc.Block() as block:
            def run(name):
                def body(e):
                    for fn, ws, inc in self.ops[name]:
                        for sem, val in ws:
                            e.wait_ge(sem, val)
                        ins = fn(e)
                        if inc is not None:
                            ins.then_inc(inc[0], inc[1])
                return body
            block.tensor(run("pe"))
            block.scalar(run("act"))
            block.vector(run("dve"))
            block.gpsimd(run("pool"))
            block.sync(run("sp"))


ASTOP = [99]
CF_L1 = [1566]
AT_RATIO = [2]


def build(nc, upto=None, dbg=False):
    p = Prog(nc)

    def din(name, shape):
        return nc.dram_tensor(name, list(shape), F32, kind="ExternalInput").ap()

    xin = din("xin", [NTOK, D])
    ck = din("ck", [DEPTH, 256, 1024])
    cv = din("cv", [DEPTH, 256, 1024])
    cvec = din("cvec", [2, D])
    w_mod = din("w_mod", [DEPTH, D, 6 * D])
    b_mod = din("b_mod", [DEPTH, 6 * D])
    g_norm1 = din("g_norm1", [DEPTH, D])
    w_in = din("w_in", [DEPTH, D, IN_COLS])
    da_lambda = din("da_lambda", [DEPTH, 256])
    da_subln = din("da_subln", [DEPTH, 128])
    w_da_out = din("w_da_out", [DEPTH, 1024, D])
    sc_conv = din("sc_conv", [DEPTH, 3, 512])
    w_sc_out = din("w_sc_out", [DEPTH, 512, D])
    cf_conv = din("cf_conv", [DEPTH, 31, 512])
    cf_conv_b = din("cf_conv_b", [DEPTH, 512])
    cf_ln_g = din("cf_ln_g", [DEPTH, 512])
    cf_ln_b = din("cf_ln_b", [DEPTH, 512])
    w_cf_out = din("w_cf_out", [DEPTH, 512, D])
    b_gate = din("b_gate", [DEPTH, 3 * D])
    w_out = din("w_out", [DEPTH, D, D])
    g_norm2 = din("g_norm2", [DEPTH, D])
    w_ffn_in = din("w_ffn_in", [DEPTH, D, 2 * FF])
    w_ffn_out = din("w_ffn_out", [DEPTH, FF, D])
    g_final = din("g_final", [1, D])
    c_ident = din("c_ident", [128, 128])
    c_perm = din("c_perm", [128, 128])
    c_cos = din("c_cos", [128, 1024])
    c_sin = din("c_sin", [128, 1024])

    y_out = nc.dram_tensor("y", [NTOK, D], F32, kind="ExternalOutput").ap()
    nk_out = nc.dram_tensor("nk", [2, DEPTH, 256, 1024], F32, kind="ExternalOutput").ap()
    nv_out = nc.dram_tensor("nv", [2, DEPTH, 256, 1024], F32, kind="ExternalOutput").ap()
    if dbg:
        xT_d = nc.dram_tensor("xT_dbg", [16, 128, NTOK], F32, kind="ExternalOutput").ap()
        hT_dbg = nc.dram_tensor("hT_dbg", [128, 16 * NTOK], BF16, kind="ExternalOutput").ap()
        ar_dbg = nc.dram_tensor("ar_dbg", [128, 23424], F32, kind="ExternalOutput").ap()
        cols_dbg = nc.dram_tensor("cols_dbg", [128, 1400], F32, kind="ExternalOutput").ap()
    else:
        xT_d = nc.dram_tensor("xT_scratch", [16, 128, NTOK], F32).ap()
    xT_v = xT_d.rearrange("j p t -> p j t")

    hT = p.sbuf("hT", [128, 16, NTOK], BF16)
    ring = [p.sbuf(f"ring{i}", [128, 8192], BF16) for i in range(3)]
    cosT = p.sbuf("cosT", [128, 1024], F32)
    sinT = p.sbuf("sinT", [128, 1024], F32)
    identf = p.sbuf("identf", [128, 128], F32)
    identb = p.sbuf("identb", [128, 128], BF16)
    onesf = p.sbuf("onesf", [128, 128], F32)
    permf = p.sbuf("permf", [128, 128], F32)
    NCOLS = 1400
    cols = p.sbuf("cols", [128, NCOLS], F32)
    s_bf = p.sbuf("s_bf", [128, 32], BF16)
    lamt = p.sbuf("lamt", [128, 2 * 8], F32)
    lamraw = p.sbuf("lamraw", [128, 2 * 256], F32)
    gsub = p.sbuf("gsub", [128, 2 * 128], F32)
    ARENA = 23424
    arena = p.sbuf("arena", [128, ARENA], F32)
    ps = p.psum("ps", [128, 8, 512], F32)

    def A32(off_b, n):
        assert off_b % 4 == 0 and off_b // 4 + n <= ARENA, (off_b, n)
        return arena[:, off_b // 4: off_b // 4 + n]

    def A16(off_b, n):
        assert off_b % 4 == 0 and n % 2 == 0 and off_b // 4 + n // 2 <= ARENA, (off_b, n)
        return arena[:, off_b // 4: off_b // 4 + n // 2].bitcast(BF16)

    colmap = {}
    coff = [0]

    def calloc(name, n):
        colmap[name] = coff[0]
        coff[0] += n
        assert coff[0] <= NCOLS
        return colmap[name]

    def C(name, i=0, n=1):
        o = colmap[name] + i
        return cols[:, o:o + n]

    S_W = [p.dsem(f"s_w{i}") for i in range(3)]
    S_L = [p.dsem(f"s_l{i}") for i in range(4)]
    S_S = [p.dsem(f"s_s{i}") for i in range(4)]
    S_C = p.dsem("s_c")
    S_P = p.dsem("s_p")
    ring_i = [0]

    pre_cache = {}

    def wload(pieces, queue="pool", key=None):
        if key is not None and key in pre_cache:
            return pre_cache.pop(key)
        si = ring_i[0] % 3
        ring_i[0] += 1
        pairs = []
        for off, src, kc, n in pieces:
            dst = ring[si][:, off:off + kc * n].rearrange("p (k n) -> p k n", k=kc)
            pairs.append((dst, src.rearrange("(k p) n -> p k n", p=128)))
        p.dma(queue, pairs, S_W[si], w=[("ring", si)])
        return si

    def prefetch(key, pieces):
        pre_cache[key] = wload(pieces)

    def rview(si, off, kc, n):
        return ring[si][:, off:off + kc * n].rearrange("p (k n) -> p k n", k=kc)

    ps_rr = [0]

    def psbank(lo=0, hi=8):
        b = lo + ps_rr[0] % (hi - lo)
        ps_rr[0] += 1
        return b

    ld_rr = [0]

    def lsem():
        ld_rr[0] += 1
        return S_L[ld_rr[0] % 4]

    st_rr = [0]

    def ssem():
        st_rr[0] += 1
        return S_S[st_rr[0] % 4]

    if dbg:
        p.op("dve", lambda e: e.memset(arena[:], 0.0), w=["arena0"])
        p.op("dve", lambda e: e.memset(cols[:], 0.0), w=["cols0"])
        p.op("pool", lambda e: e.memset(hT[:], 0.0), w=["hT0"])
        p.barrier()
    p.dma("sp", [(identf[:], c_ident)], S_C, w=["identf"])
    p.dma("sp", [(permf[:], c_perm)], lsem(), w=["permf"])
    p.dma("sp", [(cosT[:], c_cos)], lsem(), w=["cos"])
    p.dma("sp", [(sinT[:], c_sin)], lsem(), w=["sin"])
    p.op("dve", lambda e: e.memset(onesf[:], 1.0), w=["ones"])
    p.op("dve", lambda e: e.tensor_copy(out=identb[:], in_=identf[:]), r=["identf"], w=["identb"])
    for l in range(DEPTH):
        p.dma("sp", [(lamraw[:, l * 256:(l + 1) * 256], da_lambda[l:l + 1, :].partition_broadcast(128))], lsem(),
              w=[("lamraw", l)])
        p.dma("sp", [(gsub[:, l * 128:(l + 1) * 128], da_subln[l:l + 1, :].partition_broadcast(128))], lsem(),
              w=[("gsub", l)])

    stg_i = [0]

    def to_cols(name, src_rows, R):
        off = calloc(name, R)
        k = stg_i[0] % 4
        stg_i[0] += 1
        stg = A32(k * 512, 128)
        bank = psbank()
        p.dma("sp", [(stg[0:R, :], src_rows)], lsem(), w=[("stg", k)])
        p.mm([lambda e: e.transpose(out=ps[:, bank, 0:R], in_=stg[0:R, :], identity=identf[0:R, 0:R])],
             r=[("stg", k), "identf"], w=[("ps", bank)])
        p.op("dve", lambda e: e.tensor_copy(out=cols[:, off:off + R], in_=ps[:, bank, 0:R]),
             r=[("ps", bank)], w=[("col", name)])

    calloc("eps", 1)
    p.op("dve", lambda e: e.memset(C("eps"), EPS), w=[("col", "eps")])
    to_cols("cvec", cvec.rearrange("v (k p) -> (v k) p", p=128), 32)
    to_cols("gfin", g_final.rearrange("o (k p) -> (o k) p", p=128), 16)
    for l in range(DEPTH):
        to_cols(f"g1_{l}", g_norm1[l:l + 1, :].rearrange("o (k p) -> (o k) p", p=128), 16)
        to_cols(f"g2_{l}", g_norm2[l:l + 1, :].rearrange("o (k p) -> (o k) p", p=128), 16)
        to_cols(f"bg_{l}", b_gate[l:l + 1, :].rearrange("o (k p) -> (o k) p", p=128), 48)
        to_cols(f"scw_{l}", sc_conv[l].rearrange("t (k p) -> (t k) p", p=128), 12)
        to_cols(f"cfw_{l}", cf_conv[l].rearrange("t (k p) -> (t k) p", p=128), 124)
        to_cols(f"cfb_{l}", cf_conv_b[l:l + 1, :].rearrange("o (k p) -> (o k) p", p=128), 4)
        to_cols(f"lng_{l}", cf_ln_g[l:l + 1, :].rearrange("o (k p) -> (o k) p", p=128), 4)
        to_cols(f"lnb_{l}", cf_ln_b[l:l + 1, :].rearrange("o (k p) -> (o k) p", p=128), 4)
        to_cols(f"bmod_{l}", b_mod[l:l + 1, :].rearrange("o (k p) -> (o k) p", p=128), 96)
        for v in range(2):
            calloc(f"mod_{l}_{v}", 96)
            calloc(f"A1_{l}_{v}", 16)
            calloc(f"A2_{l}_{v}", 16)
    p.op("act", lambda e: e.activation(out=s_bf[:], in_=C("cvec", 0, 32), func=AF.Silu), r=[("col", "cvec")], w=["s_bf"])
    for l in range(DEPTH):
        lam_init = 0.8 - 0.6 * math.exp(-0.3 * l)
        lr = lamraw[:, l * 256:(l + 1) * 256]
        lt = lamt[:, l * 8:(l + 1) * 8]
        k0 = ("lamt", l)
        p.op("dve", lambda e, lr=lr: e.tensor_tensor(out=lr[:, 0:64], in0=lr[:, 0:64], in1=lr[:, 64:128], op=ALU.mult),
             r=[("lamraw", l)], w=[("lamraw", l)])
        p.op("dve", lambda e, lr=lr: e.tensor_tensor(out=lr[:, 128:192], in0=lr[:, 128:192], in1=lr[:, 192:256], op=ALU.mult),
             r=[("lamraw", l)], w=[("lamraw", l)])
        p.op("dve", lambda e, lr=lr, lt=lt: e.reduce_sum(out=lt[:, 0:1], in_=lr[:, 0:64], axis=mybir.AxisListType.X),
             r=[("lamraw", l)], w=[k0])
        p.op("dve", lambda e, lr=lr, lt=lt: e.reduce_sum(out=lt[:, 1:2], in_=lr[:, 128:192], axis=mybir.AxisListType.X),
             r=[("lamraw", l)], w=[k0])
        p.op("act", lambda e, lt=lt: e.activation(out=lt[:, 2:4], in_=lt[:, 0:2], func=AF.Exp), r=[k0], w=[k0])
        p.op("dve", lambda e, lt=lt, li=lam_init: e.scalar_tensor_tensor(out=lt[:, 4:5], in0=lt[:, 3:4], scalar=-li, in1=lt[:, 2:3],
                                                                       op0=ALU.add, op1=ALU.subtract), r=[k0], w=[k0])
        p.op("dve", lambda e, l=l, li=lam_init: e.tensor_scalar(out=gsub[:, l * 128:(l + 1) * 128], in0=gsub[:, l * 128:(l + 1) * 128],
                                                               scalar1=1.0 - li, scalar2=None, op0=ALU.mult),
             r=[("gsub", l)], w=[("gsub", l)])

    def x0_gen():
        for t in range(12):
            xb = A32(8192 + (t % 2) * 8192, 2048)
            sg = A32(8192 + 16384 + (t % 2) * 8192, 2048)
            p.dma("sp", [(xb, xin[t * 128:(t + 1) * 128, :])], lsem(), w=[("x0b", t % 2)])
            for half in range(2):
                b0 = 4 * half
                fns = []
                for jj in range(8):
                    j = (0 if half == 0 else 8) + jj
                    fns.append(lambda e, j=j, jj=jj, b0=b0, xb=xb: e.transpose(
                        out=ps[:, b0 + jj // 4, (jj % 4) * 128:(jj % 4 + 1) * 128], in_=xb[:, j * 128:(j + 1) * 128], identity=identf[:]))
                p.mm(fns, r=[("x0b", t % 2), "identf"], w=[("ps", b0), ("ps", b0 + 1)])
                eng = "act" if half == 0 else "dve"
                if eng == "act":
                    p.op("act", lambda e, b0=b0, sg=sg, half=half: e.activation(
                        out=sg[:, half * 1024:(half + 1) * 1024].rearrange("p (a b) -> p a b", a=2), in_=ps[:, b0:b0 + 2, :], func=AF.Copy),
                        r=[("ps", b0), ("ps", b0 + 1)], w=[("x0s", t % 2, half)])
                else:
                    p.op("dve", lambda e, b0=b0, sg=sg, half=half: e.tensor_copy(
                        out=sg[:, half * 1024:(half + 1) * 1024].rearrange("p (a b) -> p a b", a=2), in_=ps[:, b0:b0 + 2, :]),
                        r=[("ps", b0), ("ps", b0 + 1)], w=[("x0s", t % 2, half)])
            p.dma("sp", [(xT_v[:, :, t * 128:(t + 1) * 128], sg.rearrange("p (j c) -> p j c", j=16))], ssem(),
                  r=[("x0s", t % 2, 0), ("x0s", t % 2, 1)], w=[("xT0", t)])
            yield

    def mod_pieces(l, cb):
        return [(0, w_mod[l][:, cb * 512:(cb + 1) * 512], 16, 512)]

    MODBANK = 7

    def mod_load(l, cb):
        return wload(mod_pieces(l, cb), key=("mod", l, cb))

    def mod_mm(l, cb, si, bank=7):
        sview = s_bf[:].rearrange("p (v k) -> p v k", v=2)
        wv = rview(si, 0, 16, 512)
        for c in range(4):
            fns = [lambda e, kc=kc, c=c, wv=wv: e.matmul(ps[:, bank, c * 2:c * 2 + 2], lhsT=wv[:, kc, c * 128:(c + 1) * 128],
                                                          rhs=sview[:, :, kc], start=(kc == 0), stop=(kc == 15))
                   for kc in range(16)]
            p.mm(fns, r=[("ring", si), "s_bf"], w=[("ps", bank)])
        pv = ps[:, bank, 0:8].rearrange("p (q v) -> p q v", v=2)
        q0 = cb * 4
        for v in range(2):
            p.op("dve", lambda e, v=v: e.tensor_tensor(out=C(f"mod_{l}_{v}", q0, 4), in0=pv[:, :, v], in1=C(f"bmod_{l}", q0, 4), op=ALU.add),
                 r=[("ps", bank), ("col", f"bmod_{l}")], w=[("col", f"mod_{l}_{v}")])
        if cb == 7:
            mod_derive(l, 1)
        if cb == 19:
            mod_derive(l, 2)

    def mod_slab(l, cb, bank=7):
        mod_mm(l, cb, mod_load(l, cb), bank)

    def mod_derive(l, which):
        for v in range(2):
            mk = ("col", f"mod_{l}_{v}")
            if which == 1:
                p.op("dve", lambda e, v=v: e.scalar_tensor_tensor(out=C(f"A1_{l}_{v}", 0, 16), in0=C(f"mod_{l}_{v}", 16, 16), scalar=1.0,
                                                                 in1=C(f"g1_{l}", 0, 16), op0=ALU.add, op1=ALU.mult),
                     r=[mk, ("col", f"g1_{l}")], w=[("col", f"A1_{l}_{v}")])
            else:
                p.op("dve", lambda e, v=v: e.scalar_tensor_tensor(out=C(f"A2_{l}_{v}", 0, 16), in0=C(f"mod_{l}_{v}", 64, 16), scalar=1.0,
                                                                 in1=C(f"g2_{l}", 0, 16), op0=ALU.add, op1=ALU.mult),
                     r=[mk, ("col", f"g2_{l}")], w=[("col", f"A2_{l}_{v}")])

    MOD_PRE = {0: list(range(8, 16)), 1: list(range(12, 18))}
    MOD_ATT = {0: list(range(16, 24)), 1: list(range(18, 24))}

    def phase_pre(l):
        if l == 0:
            xg = x0_gen()
            for t in range(12):
                next(xg)
                if t < 8:
                    mod_slab(l, t)
            p.barrier()
        todo = list(MOD_PRE[l])
        steps = 0
        per = max(1, 56 // max(1, len(todo)))
        for _ in norm_gen(l, 1):
            steps += 1
            if steps % per == 0 and todo:
                mod_slab(l, todo.pop(0))
        while todo:
            mod_slab(l, todo.pop(0))

    def phase_n2(l):
        if l + 1 >= DEPTH:
            phase_norm(l, 2)
            return
        todo = list(range(12))
        steps = 0
        for _ in norm_gen(l, 2):
            steps += 1
            if steps % 4 == 0 and todo:
                mod_slab(l + 1, todo.pop(0))
        while todo:
            mod_slab(l + 1, todo.pop(0))

    def rstd_from_psum(bank, n, dst, scale, key):
        p.op("dve", lambda e: e.tensor_scalar(out=dst, in0=ps[:, bank, 0:n], scalar1=scale, scalar2=EPS, op0=ALU.mult, op1=ALU.add),
             r=[("ps", bank)], w=[key])
        p.op("act", lambda e: e.activation(out=dst, in_=dst, func=AF.Sqrt), r=[key], w=[key])
        p.op("dve", lambda e: e.reciprocal(out=dst, in_=dst), r=[key], w=[key])

    def phase_norm(l, which):
        for _ in norm_gen(l, which):
            pass

    def norm_gen(l, which):
        for tb in range(3):
            v = 0 if tb == 0 else 1
            An = f"A{which}_{l}_{v}"
            Bc = (f"mod_{l}_{v}", 0 if which == 1 else 48)
            xb = A32((tb % 2) * 32768, 8192).rearrange("p (j t) -> p j t", j=16)
            kx = ("nx", tb % 2)
            p.dma("sp", [(xb, xT_v[:, :, tb * 512:(tb + 1) * 512])], lsem(), w=[kx])
            bank = psbank(0, 2)
            for jg in range(4):
                sq = A32(65536 + (jg % 2) * 8192, 2048)
                ksq = ("nsq", jg % 2)
                p.op("act", lambda e, jg=jg, sq=sq, xb=xb: e.activation(out=sq, in_=xb[:, 4 * jg:4 * jg + 4, :].rearrange("p j t -> p (j t)"), func=AF.Square),
                     r=[kx], w=[ksq])
                p.mm([lambda e, jg=jg, k=k, sq=sq, bank=bank: e.matmul(ps[:, bank, :], lhsT=onesf[:], rhs=sq[:, k * 512:(k + 1) * 512],
                                                                      start=(jg == 0 and k == 0), stop=(jg == 3 and k == 3)) for k in range(4)],
                     r=[ksq, "ones"], w=[("ps", bank)])
                yield
            rs = A32(65536 + 16384, 512)
            rstd_from_psum(bank, 512, rs, 1.0 / D, "nrs")
            for j in range(16):
                tm = A32(65536 + 18432 + (j % 2) * 2048, 512)
                p.op("dve", lambda e, j=j, tm=tm, xb=xb: e.tensor_tensor(out=tm, in0=xb[:, j, :], in1=rs, op=ALU.mult),
                     r=[kx, "nrs"], w=[("ntm", j % 2)])
                p.op("act", lambda e, j=j, tm=tm, An=An, Bc=Bc, tb=tb: e.activation(
                    out=hT[:, j, tb * 512:(tb + 1) * 512], in_=tm, func=AF.Identity, scale=C(An, j), bias=C(Bc[0], Bc[1] + j)),
                    r=[("ntm", j % 2), ("col", An), ("col", Bc[0])], w=[("hT", tb)])
                yield

    def proj(bank, wv, c0, M, src, t0, n, kcs, rkeys):
        fns = [lambda e, i=i, kc=kc: e.matmul(ps[0:M, bank, 0:n], lhsT=wv[:, i, c0:c0 + M], rhs=src[:, kc, t0:t0 + n],
                                              start=(i == 0), stop=(i == len(kcs) - 1)) for i, kc in enumerate(kcs)]
        return p.mm(fns, r=rkeys, w=[("ps", bank)])

    AT_OFF = 0
    YB_OFF = 24576
    YC_OFF = 36864
    PH_OFF = 49152

    def attn_pieces(l, h):
        return [(0, w_in[l][:, h * 128:(h + 1) * 128], 16, 128),
                (2048, w_in[l][:, OFF_K + h * 128:OFF_K + (h + 1) * 128], 16, 128),
                (4096, w_in[l][:, OFF_V + h * 128:OFF_V + (h + 1) * 128], 16, 128)]

    def phase_attn(l):
        attnT = A16(AT_OFF, 8 * NTOK).rearrange("p (h t) -> p h t", h=8)
        o = 24576
        qTs, kTs, Vas = [], [], []
        for s_ in range(2):
            qTs.append(A16(o, NTOK)); o += NTOK * 2
            kTs.append(A16(o, 1792)); o += 1792 * 2
            Vas.append(A16(o, 14 * 130).rearrange("p (t e) -> p t e", t=14)); o += 14 * 130 * 2
        PT = A16(o, 10 * 2 * 512).rearrange("p (k m q) -> p k m q", k=10, m=2); o += 20480
        qf = [A32(o + i * 2048, 512) for i in range(2)]; o += 4096
        t1 = A32(o, 512); o += 2048
        t2 = A32(o, 512); o += 2048
        ckb = A16(o, 128); o += 256
        stg = [A32(o + i * 512, 128) for i in range(2)]; o += 1024
        osa = A32(o, 12 * 128).rearrange("p (t e) -> p t e", t=12); o += 6144
        obf = [A16(o + i * 256, 128) for i in range(2)]; o += 512
        rr = A32(o, 4 * 12).rearrange("p (a t) -> p a t", a=4); o += 192
        sqj = A32(o, 128); o += 512
        assert o <= ARENA * 4, o
        lt = lamt[:, l * 8:(l + 1) * 8]
        gs = gsub[:, l * 128:(l + 1) * 128]
        HK = [("hT", 0), ("hT", 1), ("hT", 2)]
        pbank = [0]

        def pb_next():
            pbank[0] += 1
            return pbank[0] % 2

        def proj_gen(h, S):
            qT, kT, Va = qTs[S], kTs[S], Vas[S]
            si = wload(attn_pieces(l, h), key=("attn", l, h))
            yield
            wq, wk, wvv = rview(si, 0, 16, 128), rview(si, 2048, 16, 128), rview(si, 4096, 16, 128)
            p.op("pool", lambda e: e.memset(Va[:, :, 128:130], 1.0), w=[("Va", S, t) for t in range(14)])
            for kind in range(2):
                wv = wq if kind == 0 else wk
                dstT = qT if kind == 0 else kT
                dkey = "qT" if kind == 0 else "kT"
                sc = 0.125 if kind == 0 else 1.0
                for tb in range(3):
                    bank = pb_next()
                    proj(bank, wv, 0, 128, hT, tb * 512, 512, list(range(16)), [("ring", si), HK[tb]])
                    if tb == 0:
                        p.op("act", lambda e, bank=bank, dstT=dstT, sc=sc: e.activation(
                            out=dstT[:, 0:512], in_=ps[:, bank, :], func=AF.Copy, scale=sc),
                            r=[("ps", bank)], w=[(dkey, S, 0)])
                    else:
                        qq = qf[tb % 2]
                        p.op("act", lambda e, bank=bank, qq=qq, sc=sc: e.activation(out=qq, in_=ps[:, bank, :], func=AF.Copy, scale=sc),
                             r=[("ps", bank)], w=[("qf", tb % 2)])
                        pb = pb_next()
                        p.mm([lambda e, qq=qq, pb=pb: e.matmul(ps[:, pb, :], lhsT=permf[:], rhs=qq, start=True, stop=True)],
                             r=[("qf", tb % 2), "permf"], w=[("ps", pb)])
                        cs = slice((tb - 1) * 512, tb * 512)
                        p.op("dve", lambda e, qq=qq, cs=cs: e.tensor_tensor(out=t1, in0=qq, in1=cosT[:, cs], op=ALU.mult),
                             r=[("qf", tb % 2), "cos"], w=["t1"])
                        p.op("dve", lambda e, cs=cs, pb=pb: e.tensor_tensor(out=t2, in0=ps[:, pb, :], in1=sinT[:, cs], op=ALU.mult),
                             r=[("ps", pb), "sin"], w=["t2"])
                        p.op("dve", lambda e, tb=tb, dstT=dstT: e.tensor_tensor(out=dstT[:, tb * 512:(tb + 1) * 512], in0=t1, in1=t2, op=ALU.add),
                             r=["t1", "t2"], w=[(dkey, S, tb)])
                    yield
                if kind == 1:
                    for t in range(4):
                        bank = pb_next()
                        fns = [lambda e, kc=kc, t=t, bank=bank, wv=wv: e.matmul(ps[:, bank, 0:128], lhsT=hT[:, kc, t * 128:(t + 1) * 128], rhs=wv[:, kc, :],
                                                                                start=(kc == 0), stop=(kc == 15)) for kc in range(16)]
                        p.mm(fns, r=[("ring", si), HK[0]], w=[("ps", bank)])
                        sg = stg[t % 2]
                        p.op("act", lambda e, bank=bank, sg=sg: e.activation(out=sg, in_=ps[:, bank, 0:128], func=AF.Copy),
                             r=[("ps", bank)], w=[("stg", t % 2)])
                        p.dma("sp", [(nk_out[t // 2, l, (t % 2) * 128:(t % 2 + 1) * 128, h * 128:(h + 1) * 128], sg)], ssem(),
                              r=[("stg", t % 2)], w=[("nk", t, h)])
                        yield
                    for t in range(2):
                        p.dma("pool", [(ckb, ck[l, t * 128:(t + 1) * 128, h * 128:(h + 1) * 128])], S_P, w=["ckb"])
                        bank = pb_next()
                        pbv = ps[:, bank, 0:64].bitcast(BF16)
                        p.mm([lambda e, pbv=pbv: e.transpose(out=pbv, in_=ckb, identity=identb[:])], r=["ckb", "identb"], w=[("ps", bank)])
                        p.op("dve", lambda e, pbv=pbv, t=t: e.tensor_copy(out=kT[:, 1536 + t * 128:1536 + (t + 1) * 128], in_=pbv),
                             r=[("ps", bank)], w=[("kT", S, 3 + t)])
                    yield
            for t in range(12):
                bank = pb_next()
                fns = [lambda e, kc=kc, t=t, bank=bank: e.matmul(ps[:, bank, 0:128], lhsT=hT[:, kc, t * 128:(t + 1) * 128], rhs=wvv[:, kc, :],
                                                                 start=(kc == 0), stop=(kc == 15)) for kc in range(16)]
                p.mm(fns, r=[("ring", si), HK[t // 4]], w=[("ps", bank)])
                if t < 4:
                    sg = stg[t % 2]
                    p.op("act", lambda e, bank=bank, sg=sg: e.activation(out=sg, in_=ps[:, bank, 0:128], func=AF.Copy),
                         r=[("ps", bank)], w=[("stg", t % 2)])
                    p.dma("sp", [(nv_out[t // 2, l, (t % 2) * 128:(t % 2 + 1) * 128, h * 128:(h + 1) * 128], sg)], ssem(),
                          r=[("stg", t % 2)], w=[("nv", t, h)])
                    p.op("dve", lambda e, sg=sg, t=t: e.tensor_copy(out=Va[:, t, 0:128], in_=sg), r=[("stg", t % 2)], w=[("Va", S, t)])
                else:
                    p.op("dve", lambda e, bank=bank, t=t: e.tensor_copy(out=Va[:, t, 0:128], in_=ps[:, bank, 0:128]), r=[("ps", bank)], w=[("Va", S, t)])
                yield
            for t in range(2):
                p.dma("pool", [(Va[:, 12 + t, 0:128], cv[l, t * 128:(t + 1) * 128, h * 128:(h + 1) * 128])], S_P, w=[("Va", S, 12 + t)])
            yield

        stb = [0]

        def attn_gen(h, S):
            qT, kT, Va = qTs[S], kTs[S], Vas[S]
            blocks = [(0, 256, [0, 1]), (256, 256, [2, 3]), (512, 512, list(range(4, 14))), (1024, 512, list(range(4, 14)))]
            for (qs, nq, ktiles) in blocks:
                nk_ = len(ktiles)
                for ki, kt in enumerate(ktiles):
                    kcol = kt * 128 if kt < 12 else 1536 + (kt - 12) * 128
                    ktb = kt // 4 if kt < 12 else 3 + (kt - 12)
                    stb[0] += 1
                    bank = 2 + 2 * (stb[0] % 2)
                    fns = [lambda e, m=m, bank=bank, kcol=kcol, qs=qs, nq=nq: e.matmul(
                        ps[:, bank + m, 0:nq], lhsT=kT[m * 64:(m + 1) * 64, kcol:kcol + 128], rhs=qT[m * 64:(m + 1) * 64, qs:qs + nq],
                        start=True, stop=True) for m in range(2)]
                    p.mm(fns, r=[("kT", S, ktb)] + [("qT", S, tb_) for tb_ in range(qs // 512, (qs + nq - 1) // 512 + 1)],
                         w=[("ps", bank), ("ps", bank + 1)])
                    for m in range(2):
                        p.op("act", lambda e, ki=ki, bank=bank, m=m, nq=nq: e.activation(out=PT[:, ki, m, 0:nq], in_=ps[:, bank + m, 0:nq], func=AF.Exp),
                             r=[("ps", bank + m)], w=[("PT", ki, m)])
                    yield
                for qt in range(nq // 128):
                    tile = (qs + qt * 128) // 128
                    bank = 6 + qt % 2
                    ov = ps[:, bank, 0:260].rearrange("p (m e) -> p m e", m=2)
                    fns = []
                    for m in range(2):
                        for ki, kt in enumerate(ktiles):
                            fns.append(lambda e, m=m, ki=ki, kt=kt, qt=qt, ov=ov, nk_=nk_: e.matmul(
                                ov[:, m, 0:129], lhsT=PT[:, ki, m, qt * 128:(qt + 1) * 128], rhs=Va[:, kt, 0:129],
                                start=(ki == 0), stop=(ki == nk_ - 1)))
                    p.mm(fns, r=[("PT", ki, m) for ki in range(nk_) for m in range(2)] + [("Va", S, kt) for kt in ktiles], w=[("ps", bank)])
                    kr = ("rr", tile)
                    p.op("dve", lambda e, ov=ov, tile=tile: e.reciprocal(out=rr[:, 0:2, tile], in_=ov[:, :, 128]), r=[("ps", bank)], w=[kr])
                    p.op("dve", lambda e, ov=ov, tile=tile: e.tensor_scalar(out=osa[:, tile, :], in0=ov[:, 1, 0:128], scalar1=rr[:, 1, tile:tile + 1],
                                                                          scalar2=lt[:, 4:5], op0=ALU.mult, op1=ALU.mult),
                         r=[("ps", bank), kr, ("lamt", l)], w=[("osa", tile)])
                    p.op("dve", lambda e, ov=ov, tile=tile: e.scalar_tensor_tensor(out=osa[:, tile, :], in0=ov[:, 0, 0:128], scalar=rr[:, 0, tile:tile + 1],
                                                                                 in1=osa[:, tile, :], op0=ALU.mult, op1=ALU.add),
                         r=[("ps", bank), kr, ("osa", tile)], w=[("osa", tile)])
                    p.op("dve", lambda e, tile=tile: e.scalar_tensor_tensor(out=sqj, in0=osa[:, tile, :], scalar=1.0, in1=osa[:, tile, :],
                                                                          op0=ALU.mult, op1=ALU.mult, accum_out=rr[:, 2, tile:tile + 1]),
                         r=[("osa", tile)], w=[("ss", tile), "sqj"])
                    yield
            p.op("act", lambda e: e.activation(out=rr[:, 3, :], in_=rr[:, 2, :], func=AF.Sqrt, scale=1.0 / 128, bias=C("eps")),
                 r=[("ss", t) for t in range(12)] + [("col", "eps")], w=["rstd"])
            p.op("dve", lambda e: e.reciprocal(out=rr[:, 3, :], in_=rr[:, 3, :]), r=["rstd"], w=["rstd"])
            for tile in range(12):
                oi = tile % 2
                p.op("dve", lambda e, oi=oi, tile=tile: e.scalar_tensor_tensor(out=obf[oi], in0=osa[:, tile, :], scalar=rr[:, 3, tile:tile + 1], in1=gs,
                                                                             op0=ALU.mult, op1=ALU.mult),
                     r=[("osa", tile), "rstd", ("gsub", l)], w=[("obf", oi)])
                tbank = 6 + oi
                tpv = ps[:, tbank, 384:448].bitcast(BF16)
                p.mm([lambda e, oi=oi, tpv=tpv: e.transpose(out=tpv, in_=obf[oi], identity=identb[:])],
                     r=[("obf", oi), "identb"], w=[("ps", tbank)])
                p.op("act", lambda e, h=h, tile=tile, tpv=tpv: e.activation(out=attnT[:, h, tile * 128:(tile + 1) * 128], in_=tpv, func=AF.Copy),
                     r=[("ps", tbank)], w=[("attnT", h)])
            yield

        for _ in proj_gen(0, 0):
            pass
        mtodo = list(MOD_ATT[l])
        for h in range(8):
            ag = attn_gen(h, h % 2)
            pg = proj_gen(h + 1, (h + 1) % 2) if h + 1 < 8 else None
            a_alive, p_alive = True, pg is not None
            mpend = None
            if p_alive:
                next(pg)
            if mtodo:
                mcb = mtodo.pop(0)
                mpend = (mcb, mod_load(l, mcb))
            while a_alive or p_alive:
                for _ in range(AT_RATIO[0]):
                    if a_alive:
                        try:
                            next(ag)
                        except StopIteration:
                            a_alive = False
                if p_alive:
                    try:
                        next(pg)
                    except StopIteration:
                        p_alive = False
            if mpend is not None:
                mod_mm(l, mpend[0], mpend[1], bank=pb_next())

    def zpos(tok):
        return 1 + tok if tok < 256 else (258 + (tok - 256) if tok < 512 else 515 + (tok - 512))

    def sc_pieces(l, i):
        return [(0, w_in[l][:, OFF_SC + i * 512:OFF_SC + (i + 1) * 512], 16, 512)]

    def phase_sc(l):
        ybT = A16(YB_OFF, 4 * NTOK).rearrange("p (c t) -> p c t", c=4)
        o = YC_OFF
        zp = [A32(o + c * 6160, 1540) for c in range(4)]; o += 4 * 6160
        yc = [A32(o + c * 6160, 1540) for c in range(4)]; o += 4 * 6160
        gcs = [A32(o + i * 2048, 512) for i in range(2)]; o += 4096
        assert o <= ARENA * 4, o
        HK = [("hT", 0), ("hT", 1), ("hT", 2)]
        for c in range(4):
            p.op("pool", lambda e, c=c: e.memset(zp[c], 0.0), w=[("zp", c)])
        s_gc = wload(sc_pieces(l, 1), key=("sc", l, 1))
        s_sx = wload(sc_pieces(l, 2), key=("sc", l, 2))
        wgc = rview(s_gc, 0, 16, 512)
        wsx = rview(s_sx, 0, 16, 512)
        for c in range(4):
            for tb in range(3):
                b1 = psbank(0, 4)
                proj(b1, wgc, c * 128, 128, hT, tb * 512, 512, list(range(16)), [("ring", s_gc), HK[tb]])
                g = gcs[tb % 2]
                p.op("act", lambda e, b1=b1, g=g: e.activation(out=g, in_=ps[:, b1, :], func=AF.Copy), r=[("ps", b1)], w=[("gcs", tb % 2)])
                b2 = psbank(0, 4)
                proj(b2, wsx, c * 128, 128, hT, tb * 512, 512, list(range(16)), [("ring", s_sx), HK[tb]])
                if tb == 0:
                    for s_ in range(2):
                        z0 = zpos(s_ * 256)
                        p.op("dve", lambda e, b2=b2, g=g, c=c, s_=s_, z0=z0: e.tensor_tensor(
                            out=zp[c][:, z0:z0 + 256], in0=ps[:, b2, s_ * 256:(s_ + 1) * 256], in1=g[:, s_ * 256:(s_ + 1) * 256], op=ALU.mult),
                            r=[("ps", b2), ("gcs", tb % 2)], w=[("zp", c)])
                else:
                    z0 = zpos(tb * 512)
                    p.op("dve", lambda e, b2=b2, g=g, c=c, z0=z0: e.tensor_tensor(out=zp[c][:, z0:z0 + 512], in0=ps[:, b2, :], in1=g, op=ALU.mult),
                         r=[("ps", b2), ("gcs", tb % 2)], w=[("zp", c)])
            L = 1538
            wc = f"scw_{l}"
            p.op("dve", lambda e, c=c: e.tensor_scalar(out=yc[c][:, 1:1 + L], in0=zp[c][:, 1:1 + L], scalar1=C(wc, 4 + c), scalar2=None, op0=ALU.mult),
                 r=[("zp", c), ("col", wc)], w=[("yc", c)])
            p.op("dve", lambda e, c=c: e.scalar_tensor_tensor(out=yc[c][:, 1:1 + L], in0=zp[c][:, 0:L], scalar=C(wc, 0 + c), in1=yc[c][:, 1:1 + L],
                                                             op0=ALU.mult, op1=ALU.add), r=[("zp", c), ("yc", c), ("col", wc)], w=[("yc", c)])
            p.op("dve", lambda e, c=c: e.scalar_tensor_tensor(out=yc[c][:, 1:1 + L], in0=zp[c][:, 2:2 + L], scalar=C(wc, 8 + c), in1=yc[c][:, 1:1 + L],
                                                             op0=ALU.mult, op1=ALU.add), r=[("zp", c), ("yc", c), ("col", wc)], w=[("yc", c)])
        s_gb = wload([(0, w_in[l][:, OFF_SC:OFF_SC + 512], 16, 512)])
        wgb = rview(s_gb, 0, 16, 512)
        for c in range(4):
            for tb in range(3):
                b1 = psbank(0, 4)
                proj(b1, wgb, c * 128, 128, hT, tb * 512, 512, list(range(16)), [("ring", s_gb), HK[tb]])
                if tb == 0:
                    for s_ in range(2):
                        z0 = zpos(s_ * 256)
                        p.op("dve", lambda e, b1=b1, c=c, s_=s_, z0=z0: e.tensor_tensor(
                            out=ybT[:, c, s_ * 256:(s_ + 1) * 256], in0=ps[:, b1, s_ * 256:(s_ + 1) * 256], in1=yc[c][:, z0:z0 + 256], op=ALU.mult),
                            r=[("ps", b1), ("yc", c)], w=[("ybT", c)])
                else:
                    z0 = zpos(tb * 512)
                    p.op("dve", lambda e, b1=b1, c=c, tb=tb, z0=z0: e.tensor_tensor(
                        out=ybT[:, c, tb * 512:(tb + 1) * 512], in0=ps[:, b1, :], in1=yc[c][:, z0:z0 + 512], op=ALU.mult),
                        r=[("ps", b1), ("yc", c)], w=[("ybT", c)])

    def cpos(tok):
        return 15 + tok if tok < 256 else (286 + (tok - 256) if tok < 512 else 557 + (tok - 512))

    def cf_pieces(l, i):
        return [(0, w_in[l][:, OFF_CF + i * 512:OFF_CF + (i + 1) * 512], 16, 512)]

    def phase_cf(l):
        ycT = A16(YC_OFF, 4 * NTOK).rearrange("p (c t) -> p c t", c=4)
        dg = A16(YC_OFF, 31 * 128).rearrange("p (j m) -> p j m", j=31)
        o = PH_OFF
        zb = [A16(o + i * 3200, 1596) for i in range(2)]; o += 6400
        cy = A32(o, 4 * NTOK).rearrange("p (c t) -> p c t", c=4); o += 4 * NTOK * 4
        sg = [A32(o + i * 2048, 512) for i in range(2)]; o += 4096
        mean = A32(o, 512); o += 2048
        rstd = A32(o, 512); o += 2048
        assert o <= ARENA * 4, o
        HK = [("hT", 0), ("hT", 1), ("hT", 2)]
        s_ca = wload(cf_pieces(l, 0), key=("cf", l, 0))
        s_cb = wload(cf_pieces(l, 1), key=("cf", l, 1))
        wca = rview(s_ca, 0, 16, 512)
        wcb = rview(s_cb, 0, 16, 512)
        wc = f"cfw_{l}"
        for c in range(4):
            z = zb[c % 2]
            kz = ("zb", c % 2)
            p.op("pool", lambda e, z=z: e.memset(z, 0.0), w=[kz])
            for tb in range(3):
                b1 = psbank(0, 4)
                proj(b1, wcb, c * 128, 128, hT, tb * 512, 512, list(range(16)), [("ring", s_cb), HK[tb]])
                g = sg[tb % 2]
                p.op("act", lambda e, b1=b1, g=g: e.activation(out=g, in_=ps[:, b1, :], func=AF.Sigmoid), r=[("ps", b1)], w=[("sg", tb % 2)])
                b2 = psbank(0, 4)
                proj(b2, wca, c * 128, 128, hT, tb * 512, 512, list(range(16)), [("ring", s_ca), HK[tb]])
                if tb == 0:
                    for s_ in range(2):
                        z0 = cpos(s_ * 256)
                        p.op("dve", lambda e, b2=b2, g=g, s_=s_, z0=z0, z=z: e.tensor_tensor(
                            out=z[:, z0:z0 + 256], in0=ps[:, b2, s_ * 256:(s_ + 1) * 256], in1=g[:, s_ * 256:(s_ + 1) * 256], op=ALU.mult),
                            r=[("ps", b2), ("sg", tb % 2)], w=[kz])
                else:
                    z0 = cpos(tb * 512)
                    p.op("dve", lambda e, b2=b2, g=g, z0=z0, z=z: e.tensor_tensor(out=z[:, z0:z0 + 512], in0=ps[:, b2, :], in1=g, op=ALU.mult),
                         r=[("ps", b2), ("sg", tb % 2)], w=[kz])
            for j in range(31):
                p.op("dve", lambda e, j=j, c=c: e.tensor_scalar(out=dg[:, j, :], in0=identb[:], scalar1=C(wc, j * 4 + c), scalar2=None, op0=ALU.mult),
                     r=["identb", ("col", wc)], w=[("dg", j)])
            segs = [(0, 256), (256, 256), (512, 512), (1024, 512)]
            for (t0, n) in segs:
                bank = 4 + psbank(0, 4)
                p0 = cpos(t0) - 15
                fns = [lambda e, j=j, bank=bank, n=n, p0=p0, z=z: e.matmul(ps[:, bank, 0:n], lhsT=dg[:, j, :], rhs=z[:, p0 + j:p0 + j + n],
                                                                          start=(j == 0), stop=(j == 30)) for j in range(31)]
                p.mm(fns, r=[kz] + [("dg", j) for j in range(31)], w=[("ps", bank)])
                p.op("act", lambda e, bank=bank, n=n, t0=t0, c=c: e.activation(out=cy[:, c, t0:t0 + n], in_=ps[:, bank, 0:n], func=AF.Identity,
                                                                              bias=C(f"cfb_{l}", c)),
                     r=[("ps", bank), ("col", f"cfb_{l}")], w=[("cy", c)])
        for tb in range(3):
            ts_ = slice(tb * 512, (tb + 1) * 512)
            bm = psbank(0, 4)
            bq = psbank(0, 4)
            for c in range(4):
                p.mm([lambda e, c=c, bm=bm, ts_=ts_: e.matmul(ps[:, bm, :], lhsT=onesf[:], rhs=cy[:, c, ts_], start=(c == 0), stop=(c == 3))],
                     r=[("cy", c), "ones"], w=[("ps", bm)])
            for c in range(4):
                sq = sg[c % 2]
                p.op("act", lambda e, c=c, sq=sq, ts_=ts_: e.activation(out=sq, in_=cy[:, c, ts_], func=AF.Square), r=[("cy", c)], w=[("sg", c % 2)])
                p.mm([lambda e, c=c, sq=sq, bq=bq: e.matmul(ps[:, bq, :], lhsT=onesf[:], rhs=sq, start=(c == 0), stop=(c == 3))],
                     r=[("sg", c % 2), "ones"], w=[("ps", bq)])
            p.op("dve", lambda e, bm=bm: e.tensor_scalar(out=mean, in0=ps[:, bm, :], scalar1=1.0 / 512, scalar2=None, op0=ALU.mult),
                 r=[("ps", bm)], w=["mean"])
            p.op("dve", lambda e: e.tensor_tensor(out=rstd, in0=mean, in1=mean, op=ALU.mult), r=["mean"], w=["rstd"])
            p.op("dve", lambda e, bq=bq: e.scalar_tensor_tensor(out=rstd, in0=ps[:, bq, :], scalar=1.0 / 512, in1=rstd, op0=ALU.mult, op1=ALU.subtract),
                 r=[("ps", bq), "rstd"], w=["rstd"])
            p.op("act", lambda e: e.activation(out=rstd, in_=rstd, func=AF.Sqrt, bias=C("eps")), r=["rstd", ("col", "eps")], w=["rstd"])
            p.op("dve", lambda e: e.reciprocal(out=rstd, in_=rstd), r=["rstd"], w=["rstd"])
            for c in range(4):
                tmp = sg[c % 2]
                p.op("dve", lambda e, c=c, tmp=tmp, ts_=ts_: e.tensor_tensor(out=tmp, in0=cy[:, c, ts_], in1=mean, op=ALU.subtract),
                     r=[("cy", c), "mean"], w=[("sg", c % 2)])
                p.op("dve", lambda e, tmp=tmp: e.tensor_tensor(out=tmp, in0=tmp, in1=rstd, op=ALU.mult), r=[("sg", c % 2), "rstd"], w=[("sg", c % 2)])
                p.op("act", lambda e, c=c, tb=tb, tmp=tmp: e.activation(out=ycT[:, c, tb * 512:(tb + 1) * 512], in_=tmp, func=AF.Silu,
                                                                       scale=C(f"lng_{l}", c), bias=C(f"lnb_{l}", c)),
                     r=[("sg", c % 2), ("col", f"lng_{l}"), ("col", f"lnb_{l}")], w=[("ycT", c)] + [("dg", j) for j in range(31)])

    def xsweep(blocks, xs, banks):
        NB = len(xs)
        LA = NB - 2
        nb = len(blocks)

        def load(i):
            j, tb = blocks[i][0], blocks[i][1]
            p.dma("sp", [(xs[i % NB], xT_d[j, :, tb * 512:(tb + 1) * 512])], lsem(), r=[("xT", j, tb)], w=[("xblk", i % NB)])
        for i in range(min(LA, nb)):
            load(i)
        for i, (j, tb, gcol, pf, pre) in enumerate(blocks):
            if pre is not None:
                pre()
            bank = banks[i % len(banks)]
            pf(bank)
            if i + LA < nb:
                load(i + LA)
            xb_ = xs[i % NB]
            xk = ("xblk", i % NB)
            p.op("dve", lambda e, xb_=xb_, bank=bank, gcol=gcol: e.scalar_tensor_tensor(out=xb_, in0=ps[:, bank, :], scalar=gcol, in1=xb_,
                                                                                     op0=ALU.mult, op1=ALU.add),
                 r=[("ps", bank), xk], w=[xk])
            p.dma("sp", [(xT_d[j, :, tb * 512:(tb + 1) * 512], xb_)], ssem(), r=[xk], w=[("xT", j, tb)])

    def mg_pieces(l, j0, which):
        cbase = j0 * 128
        g0 = OFF_GATE + cbase
        if which == 0:
            return [(0, w_in[l][:, g0:g0 + 256], 16, 256), (4096, w_in[l][:, g0 + D:g0 + D + 256], 16, 256)]
        return [(0, w_in[l][:, g0 + 2 * D:g0 + 2 * D + 256], 16, 256),
                (4096, w_da_out[l][:, cbase:cbase + 256], 8, 256),
                (6144, w_sc_out[l][:, cbase:cbase + 256], 4, 256),
                (7168, w_cf_out[l][:, cbase:cbase + 256], 4, 256)]

    def phase_merge(l):
        attnT = A16(AT_OFF, 8 * NTOK).rearrange("p (h t) -> p h t", h=8)
        ybT = A16(YB_OFF, 4 * NTOK).rearrange("p (c t) -> p c t", c=4)
        ycT = A16(YC_OFF, 4 * NTOK).rearrange("p (c t) -> p c t", c=4)
        o = PH_OFF
        mT = A16(o, 8 * NTOK).rearrange("p (c t) -> p c t", c=8); o += 24576
        sa = [A32(o + i * 2048, 512) for i in range(2)]; o += 4096
        mm_ = A32(o, 512); o += 2048
        xs = [A32(o + i * 2048, 512) for i in range(4)]; o += 8192
        assert o <= ARENA * 4, o
        HK = [("hT", 0), ("hT", 1), ("hT", 2)]
        for half in range(2):
            for jq in range(4):
                j0 = half * 8 + jq * 2
                cbase = j0 * 128
                g0 = OFF_GATE + cbase
                sX = wload(mg_pieces(l, j0, 0), key=("mg", l, j0, 0))
                sY = wload(mg_pieces(l, j0, 1), key=("mg", l, j0, 1))
                gates = [rview(sX, 0, 16, 256), rview(sX, 4096, 16, 256), rview(sY, 0, 16, 256)]
                gsl = [sX, sX, sY]
                brs = [(rview(sY, 4096, 8, 256), attnT, 8, [("attnT", h) for h in range(8)]),
                       (rview(sY, 6144, 4, 256), ybT, 4, [("ybT", c) for c in range(4)]),
                       (rview(sY, 7168, 4, 256), ycT, 4, [("ycT", c) for c in range(4)])]
                for c in range(2):
                    j = j0 + c
                    for tb in range(3):
                        for g in range(3):
                            b1 = psbank(0, 4)
                            proj(b1, gates[g], c * 128, 128, hT, tb * 512, 512, list(range(16)), [("ring", gsl[g]), HK[tb]])
                            s_ = sa[g % 2]
                            p.op("act", lambda e, b1=b1, s_=s_, g=g, j=j: e.activation(out=s_, in_=ps[:, b1, :], func=AF.Sigmoid,
                                                                                      bias=C(f"bg_{l}", g * 16 + j)),
                                 r=[("ps", b1), ("col", f"bg_{l}")], w=[("sa", g % 2)])
                            b2 = psbank(0, 4)
                            wv, src, nkc, skeys = brs[g]
                            proj(b2, wv, c * 128, 128, src, tb * 512, 512, list(range(nkc)), [("ring", sY)] + skeys)
                            p.op("dve", lambda e, b2=b2, s_=s_: e.tensor_tensor(out=s_, in0=ps[:, b2, :], in1=s_, op=ALU.mult),
                                 r=[("ps", b2), ("sa", g % 2)], w=[("sa", g % 2)])
                            if g == 0:
                                p.op("dve", lambda e, s_=s_: e.tensor_copy(out=mm_, in_=s_), r=[("sa", 0)], w=["mm"])
                            elif g == 1:
                                p.op("dve", lambda e, s_=s_: e.tensor_tensor(out=mm_, in0=mm_, in1=s_, op=ALU.add), r=[("sa", 1), "mm"], w=["mm"])
                            else:
                                p.op("dve", lambda e, s_=s_, jq=jq, c=c, tb=tb: e.tensor_tensor(
                                    out=mT[:, jq * 2 + c, tb * 512:(tb + 1) * 512], in0=mm_, in1=s_, op=ALU.add),
                                    r=[("sa", 0), "mm"], w=[("mT", jq * 2 + c)])
            blocks = []
            st = {}
            for n in range(4):
                def pre(n=n, half=half):
                    si = wload([(0, w_out[l][half * 1024:(half + 1) * 1024, n * 512:(n + 1) * 512], 8, 512)])
                    st["si"] = si
                    st["wv"] = rview(si, 0, 8, 512)
                for c in range(4):
                    j = n * 4 + c
                    for tb in range(3):
                        v = 0 if tb == 0 else 1

                        def pf(bank, c=c, tb=tb):
                            proj(bank, st["wv"], c * 128, 128, mT, tb * 512, 512, list(range(8)), [("ring", st["si"])] + [("mT", k) for k in range(8)])
                        blocks.append((j, tb, C(f"mod_{l}_{v}", 32 + j), pf, pre if (c == 0 and tb == 0) else None))
            xsweep(blocks, xs, [4, 5, 6, 7])

    def ffn_pieces(l, cc, which):
        return [(0, w_ffn_in[l][:, which * FF + cc:which * FF + cc + 512], 16, 512)]

    def phase_ffn(l):
        o = 0
        gT = A16(o, 12 * NTOK).rearrange("p (c t) -> p c t", c=12); o += 36864
        su = [A32(o + i * 2048, 512) for i in range(2)]; o += 4096
        xs = [A32(o + i * 2048, 512) for i in range(6)]; o += 6 * 2048
        HK = [("hT", 0), ("hT", 1), ("hT", 2)]
        groups = [(0, 12), (12, 12), (24, 12), (36, 8)]
        for (c0, nch) in groups:
            for sl in range(nch // 4):
                cc = (c0 + sl * 4) * 128
                s_u = wload(ffn_pieces(l, cc, 0), key=("ffn", l, cc, 0))
                s_w = wload(ffn_pieces(l, cc, 1), key=("ffn", l, cc, 1))
                wu = rview(s_u, 0, 16, 512)
                ww = rview(s_w, 0, 16, 512)
                for c in range(4):
                    for tb in range(3):
                        b1 = psbank(0, 4)
                        proj(b1, wu, c * 128, 128, hT, tb * 512, 512, list(range(16)), [("ring", s_u), HK[tb]])
                        s_ = su[tb % 2]
                        p.op("act", lambda e, b1=b1, s_=s_: e.activation(out=s_, in_=ps[:, b1, :], func=AF.Silu), r=[("ps", b1)], w=[("su", tb % 2)])
                        b2 = psbank(0, 4)
                        proj(b2, ww, c * 128, 128, hT, tb * 512, 512, list(range(16)), [("ring", s_w), HK[tb]])
                        p.op("dve", lambda e, b2=b2, s_=s_, sl=sl, c=c, tb=tb: e.tensor_tensor(
                            out=gT[:, sl * 4 + c, tb * 512:(tb + 1) * 512], in0=ps[:, b2, :], in1=s_, op=ALU.mult),
                            r=[("ps", b2), ("su", tb % 2)], w=[("gT", sl * 4 + c)])
            blocks = []
            st = {}
            for n in range(4):
                def pre(n=n, c0=c0, nch=nch):
                    si = wload([(0, w_ffn_out[l][c0 * 128:(c0 + nch) * 128, n * 512:(n + 1) * 512], nch, 512)])
                    st["si"] = si
                    st["wv"] = rview(si, 0, nch, 512)
                for c in range(4):
                    j = n * 4 + c
                    for tb in range(3):
                        v = 0 if tb == 0 else 1

                        def pf(bank, c=c, tb=tb, nch=nch):
                            proj(bank, st["wv"], c * 128, 128, gT, tb * 512, 512, list(range(nch)), [("ring", st["si"])] + [("gT", k) for k in range(nch)])
                        blocks.append((j, tb, C(f"mod_{l}_{v}", 80 + j), pf, pre if (c == 0 and tb == 0) else None))
            xsweep(blocks, xs, [4, 5, 6, 7])

    def phase_final():
        for t in range(12):
            xb = A32((t % 2) * 8192, 2048).rearrange("p (j c) -> p j c", j=16)
            yb = A32(16384 + (t % 2) * 8192, 2048)
            kx = ("fx", t % 2)
            p.dma("sp", [(xb, xT_v[:, :, t * 128:(t + 1) * 128])], lsem(), w=[kx])
            bank = 0 if t % 2 == 0 else 5
            sq = A32(32768 + (t % 2) * 8192, 2048)
            ksq = ("fsq", t % 2)
            p.op("act", lambda e, sq=sq, xb=xb: e.activation(out=sq, in_=xb.rearrange("p j c -> p (j c)"), func=AF.Square), r=[kx], w=[ksq])
            p.mm([lambda e, j=j, sq=sq, bank=bank: e.matmul(ps[:, bank, 0:128], lhsT=onesf[:], rhs=sq[:, j * 128:(j + 1) * 128],
                                                           start=(j == 0), stop=(j == 15)) for j in range(16)],
                 r=[ksq, "ones"], w=[("ps", bank)])
            rs = A32(49152 + (t % 2) * 512, 128)
            krs = ("frs", t % 2)
            rstd_from_psum(bank, 128, rs, 1.0 / D, krs)
            for j in range(16):
                p.op("dve", lambda e, j=j, xb=xb, rs=rs: e.scalar_tensor_tensor(out=xb[:, j, :], in0=xb[:, j, :], scalar=C("gfin", j), in1=rs,
                                                                              op0=ALU.mult, op1=ALU.mult),
                     r=[kx, krs, ("col", "gfin")], w=[kx])
            for half in range(2):
                b0 = 1 + 2 * half + (t % 2) * 0
                fns = []
                for jj in range(8):
                    j = half * 8 + jj
                    fns.append(lambda e, j=j, jj=jj, b0=b0, xb=xb: e.transpose(
                        out=ps[:, b0 + jj // 4, (jj % 4) * 128:(jj % 4 + 1) * 128], in_=xb[:, j, :], identity=identf[:]))
                p.mm(fns, r=[kx, "identf"], w=[("ps", b0), ("ps", b0 + 1)])
                if half == 0:
                    p.op("act", lambda e, b0=b0, yb=yb: e.activation(out=yb[:, 0:1024].rearrange("p (a b) -> p a b", a=2), in_=ps[:, b0:b0 + 2, :], func=AF.Copy),
                         r=[("ps", b0), ("ps", b0 + 1)], w=[("fy", t % 2, 0)])
                else:
                    p.op("dve", lambda e, b0=b0, yb=yb: e.tensor_copy(out=yb[:, 1024:2048].rearrange("p (a b) -> p a b", a=2), in_=ps[:, b0:b0 + 2, :]),
                         r=[("ps", b0), ("ps", b0 + 1)], w=[("fy", t % 2, 1)])
            p.dma("sp", [(y_out[t * 128:(t + 1) * 128, :], yb)], ssem(), r=[("fy", t % 2, 0), ("fy", t % 2, 1)], w=[("yout", t)])

    plan = []
    for l in range(DEPTH):
        plan += [(f"n1_{l}", lambda l=l: phase_pre(l), lambda l=l: [prefetch(("mod", l, cb), mod_pieces(l, cb)) for cb in ((0, 1) if l == 0 else (12, 13))]),
                 (f"attn{l}", lambda l=l: phase_attn(l), lambda l=l: prefetch(("attn", l, 0), attn_pieces(l, 0))),
                 (f"sc{l}", lambda l=l: phase_sc(l), lambda l=l: [prefetch(("sc", l, i), sc_pieces(l, i)) for i in (1, 2)]),
                 (f"cf{l}", lambda l=l: phase_cf(l), lambda l=l: [prefetch(("cf", l, i), cf_pieces(l, i)) for i in (0, 1)]),
                 (f"merge{l}", lambda l=l: phase_merge(l), lambda l=l: [prefetch(("mg", l, 0, i), mg_pieces(l, 0, i)) for i in (0, 1)]),
                 (f"n2_{l}", lambda l=l: phase_n2(l), (lambda l=l: [prefetch(("mod", l + 1, cb), mod_pieces(l + 1, cb)) for cb in range(2)]) if l + 1 < DEPTH else None),
                 (f"ffn{l}", lambda l=l: phase_ffn(l), lambda l=l: [prefetch(("ffn", l, 0, i), ffn_pieces(l, 0, i)) for i in (0, 1)])]
    plan += [("final", phase_final, None)]
    p.barrier()
    hooked = set()
    for pi, (name, fn, hook) in enumerate(plan):
        fn()
        if not (upto is not None and name == upto):
            for pj in range(pi + 1, min(pi + 3, len(plan))):
                nh = plan[pj][2]
                if nh is not None:
                    if pj not in hooked:
                        hooked.add(pj)
                        nh()
                    break
        p.barrier()
        if upto is not None and name == upto:
            break
    if dbg:
        p.dma("sp", [(hT_dbg, hT[:].rearrange("p j t -> p (j t)"))], S_S[0])
        p.dma("sp", [(ar_dbg, arena[:])], S_S[1])
        p.dma("sp", [(cols_dbg, cols[:])], S_S[2])
    p.barrier(full=True)
    p.emit()
    p.stack.close()
    return nc


def _consts():
    ident = np.eye(128, dtype=np.float32)
    perm = np.zeros((128, 128), dtype=np.float32)
    sign = np.zeros(128, dtype=np.float32)
    for m in range(128):
        d = m % 64
        blk = d % 32
        if blk < 16:
            partner = m + 16
            sign[m] = -1.0
        else:
            partner = m - 16
            sign[m] = 1.0
        perm[partner, m] = 1.0
    n_freq = 16
    inv = (10000.0 ** (-np.arange(n_freq, dtype=np.float32) / n_freq)).astype(np.float32)
    tok = np.arange(1024)
    row_ids = (tok // 64).astype(np.float32)
    col_ids = (tok % 64).astype(np.float32)
    ang_row = row_ids[:, None] * inv[None, :]
    ang_col = col_ids[:, None] * inv[None, :]
    cos = np.zeros((128, 1024), dtype=np.float32)
    sin = np.zeros((128, 1024), dtype=np.float32)
    for m in range(128):
        d = m % 64
        ang = ang_row if d < 32 else ang_col
        i = d % 16
        cos[m] = np.cos(ang[:, i].astype(np.float32))
        sin[m] = np.sin(ang[:, i].astype(np.float32)) * sign[m]
    return ident, perm, cos.astype(np.float32), sin.astype(np.float32)


_NC_CACHE = {}


def kernel(x_prompt, x_sample, cache_k, cache_v, c, c_ctx, w_mod, b_mod, g_norm1, w_in,
           da_lambda, da_subln, w_da_out, sc_conv, w_sc_out, cf_conv, cf_conv_b, cf_ln_g,
           cf_ln_b, w_cf_out, b_gate, w_out, g_norm2, w_ffn_in, w_ffn_out, g_final):
    f = lambda a: np.ascontiguousarray(np.asarray(a, dtype=np.float32))
    x_prompt, x_sample, cache_k, cache_v, c, c_ctx = map(f, (x_prompt, x_sample, cache_k, cache_v, c, c_ctx))
    ident, perm, cos, sin = _consts()
    shared = dict(
        w_mod=f(w_mod), b_mod=f(b_mod), g_norm1=f(g_norm1), w_in=f(w_in), da_lambda=f(da_lambda).reshape(DEPTH, 256),
        da_subln=f(da_subln), w_da_out=f(w_da_out), sc_conv=f(sc_conv), w_sc_out=f(w_sc_out), cf_conv=f(cf_conv),
        cf_conv_b=f(cf_conv_b), cf_ln_g=f(cf_ln_g), cf_ln_b=f(cf_ln_b), w_cf_out=f(w_cf_out), b_gate=f(b_gate),
        w_out=f(w_out), g_norm2=f(g_norm2), w_ffn_in=f(w_ffn_in), w_ffn_out=f(w_ffn_out), g_final=f(g_final).reshape(1, D),
        c_ident=ident, c_perm=perm, c_cos=cos, c_sin=sin)
    in_maps = []
    for i in range(8):
        xin = np.concatenate([x_prompt[2 * i], x_prompt[2 * i + 1], x_sample[i]], axis=0)
        m = dict(shared)
        m["xin"] = np.ascontiguousarray(xin)
        m["ck"] = np.ascontiguousarray(cache_k[i].reshape(DEPTH, 256, 1024))
        m["cv"] = np.ascontiguousarray(cache_v[i].reshape(DEPTH, 256, 1024))
        m["cvec"] = np.ascontiguousarray(np.stack([c_ctx, c[i]], axis=0))
        in_maps.append(m)
    nc = bass.Bass("TRN2", target_bir_lowering=False)
    build(nc)
    res = run_bass_kernel_spmd(nc, in_maps, core_ids=list(range(8)))
    y_prompt = np.zeros((16, 256, D), np.float32)
    y_sample = np.zeros((8, 1024, D), np.float32)
    new_k = np.zeros((16, DEPTH, 256, 8, 128), np.float32)
    new_v = np.zeros((16, DEPTH, 256, 8, 128), np.float32)
    for i in range(8):
        r = res.results[i]
        y = r["y"]
        y_prompt[2 * i] = y[0:256]
        y_prompt[2 * i + 1] = y[256:512]
        y_sample[i] = y[512:1536]
        new_k[2 * i:2 * i + 2] = r["nk"].reshape(2, DEPTH, 256, 8, 128)
        new_v[2 * i:2 * i + 2] = r["nv"].reshape(2, DEPTH, 256, 8, 128)
    return (y_prompt, y_sample, new_k, new_v)
```

```python
import contextlib
import math
import numpy as np
import concourse.bass as bass
import concourse.mybir as mybir
from concourse.bass_utils import run_bass_kernel_spmd

F32 = mybir.dt.float32
BF16 = mybir.dt.bfloat16
AF = mybir.ActivationFunctionType
ALU = mybir.AluOpType

D = 2048
NTOK = 1536
DEPTH = 2
FF = 5632
EPS = 1e-6
OFF_K, OFF_V, OFF_SC, OFF_CF, OFF_GATE = 1024, 2048, 3072, 4608, 5632
IN_COLS = 11776
ENGS = ("pe", "act", "dve", "pool", "sp")


class Prog:
    def __init__(self, nc):
        self.nc = nc
        self.stack = contextlib.ExitStack()
        self.ops = {e: [] for e in ENGS}
        self.esem = {}
        self.ecnt = {e: 0 for e in ENGS}
        for e in ("pe", "act", "dve", "pool"):
            self.esem[e] = self.stack.enter_context(nc.semaphore("es_" + e))
        self.dsems = {}
        self.waited = {e: {} for e in ENGS}
        self.last_w = {}
        self.readers = {}
        self.semobj = {}

    def sbuf(self, name, shape, dtype):
        return self.stack.enter_context(self.nc.sbuf_tensor(name, list(shape), dtype))

    def psum(self, name, shape, dtype):
        return self.stack.enter_context(self.nc.psum_tensor(name, list(shape), dtype))

    def dsem(self, name):
        s = self.stack.enter_context(self.nc.semaphore(name))
        self.dsems[name] = [s, 0, None]
        return name

    def _deps(self, r, w):
        evs = []
        for k in r:
            evs.append(self.last_w.get(k))
        for k in w:
            evs.append(self.last_w.get(k))
            evs.extend(self.readers.get(k, {}).values())
        return evs

    def _commit(self, ev, r, w):
        for k in r:
            d = self.readers.setdefault(k, {})
            old = d.get(id(ev[0]))
            if old is None or old[1] < ev[1]:
                d[id(ev[0])] = ev
        for k in w:
            self.last_w[k] = ev
            self.readers[k] = {}

    def _filter(self, eng, evs):
        ws = []
        seen = self.waited[eng]
        for ev in evs:
            if ev is None:
                continue
            sem, val = ev[0], ev[1]
            k = id(sem)
            if seen.get(k, 0) >= val:
                continue
            seen[k] = val
            self.semobj[k] = sem
            ws.append((sem, val))
        return ws

    def op(self, eng, fn, r=(), w=(), extra=()):
        ws = self._filter(eng, self._deps(r, w) + list(extra))
        self.ecnt[eng] += 1
        ev = (self.esem[eng], self.ecnt[eng])
        self.ops[eng].append((fn, ws, (self.esem[eng], 1)))
        self._commit(ev, r, w)
        return ev

    def mm(self, fns, r=(), w=()):
        ws = self._filter("pe", self._deps(r, w))
        n = len(fns)
        ev = None
        for i, fn in enumerate(fns):
            inc = None
            if i == n - 1:
                self.ecnt["pe"] += 1
                ev = (self.esem["pe"], self.ecnt["pe"])
                inc = (self.esem["pe"], 1)
            self.ops["pe"].append((fn, ws if i == 0 else [], inc))
        self._commit(ev, r, w)
        return ev

    def dma(self, queue, pairs, sem, r=(), w=(), **kw):
        rec = self.dsems[sem]
        evs = self._deps(r, w)
        if rec[2] is not None:
            evs.append(rec[2])
        ws = self._filter(queue, evs)
        for i, (o, i_) in enumerate(pairs):
            rec[1] += 16
            self.ops[queue].append(
                (lambda e, o=o, i_=i_: e.dma_start(out=o, in_=i_, **kw), ws if i == 0 else [], (rec[0], 16)))
        ev = (rec[0], rec[1])
        rec[2] = ev
        self._commit(ev, r, w)
        return ev

    def barrier(self, full=False):
        evs = [(self.esem[e], self.ecnt[e]) for e in ("pe", "act", "dve", "pool") if self.ecnt[e] > 0]
        for name, rec in self.dsems.items():
            if rec[2] is not None and (full or not name.startswith("s_w")):
                evs.append(rec[2])
        for e in ENGS:
            ws = self._filter(e, evs)
            if ws:
                self.ops[e].append((lambda en: en.nop(), ws, None))
        keep_w = {k: v for k, v in self.last_w.items() if isinstance(k, tuple) and k[0] == "ring"}
        keep_r = {k: v for k, v in self.readers.items() if isinstance(k, tuple) and k[0] == "ring"}
        self.last_w = {} if full else keep_w
        self.readers = {} if full else keep_r

    def emit(self):
        nc = self.nc
        with nc.Block() as block:
            def run(name):
                def body(e):
                    for fn, ws, inc in self.ops[name]:
                        for sem, val in ws:
                            e.wait_ge(sem, val)
                        ins = fn(e)
                        if inc is not None:
                            ins.then_inc(inc[0], inc[1])
                return body
            block.tensor(run("pe"))
            block.scalar(run("act"))
            block.vector(run("dve"))
            block.gpsimd(run("pool"))
            block.sync(run("sp"))


ASTOP = [99]
CF_L1 = [1566]
AT_RATIO = [2]


def build(nc, upto=None, dbg=False):
    p = Prog(nc)

    def din(name, shape):
        return nc.dram_tensor(name, list(shape), F32, kind="ExternalInput").ap()

    xin = din("xin", [NTOK, D])
    ck = din("ck", [DEPTH, 256, 1024])
    cv = din("cv", [DEPTH, 256, 1024])
    cvec = din("cvec", [2, D])
    w_mod = din("w_mod", [DEPTH, D, 6 * D])
    b_mod = din("b_mod", [DEPTH, 6 * D])
    g_norm1 = din("g_norm1", [DEPTH, D])
    w_in = din("w_in", [DEPTH, D, IN_COLS])
    da_lambda = din("da_lambda", [DEPTH, 256])
    da_subln = din("da_subln", [DEPTH, 128])
    w_da_out = din("w_da_out", [DEPTH, 1024, D])
    sc_conv = din("sc_conv", [DEPTH, 3, 512])
    w_sc_out = din("w_sc_out", [DEPTH, 512, D])
    cf_conv = din("cf_conv", [DEPTH, 31, 512])
    cf_conv_b = din("cf_conv_b", [DEPTH, 512])
    cf_ln_g = din("cf_ln_g", [DEPTH, 512])
    cf_ln_b = din("cf_ln_b", [DEPTH, 512])
    w_cf_out = din("w_cf_out", [DEPTH, 512, D])
    b_gate = din("b_gate", [DEPTH, 3 * D])
    w_out = din("w_out", [DEPTH, D, D])
    g_norm2 = din("g_norm2", [DEPTH, D])
    w_ffn_in = din("w_ffn_in", [DEPTH, D, 2 * FF])
    w_ffn_out = din("w_ffn_out", [DEPTH, FF, D])
    g_final = din("g_final", [1, D])
    c_ident = din("c_ident", [128, 128])
    c_perm = din("c_perm", [128, 128])
    c_cos = din("c_cos", [128, 1024])
    c_sin = din("c_sin", [128, 1024])

    y_out = nc.dram_tensor("y", [NTOK, D], F32, kind="ExternalOutput").ap()
    nk_out = nc.dram_tensor("nk", [2, DEPTH, 256, 1024], F32, kind="ExternalOutput").ap()
    nv_out = nc.dram_tensor("nv", [2, DEPTH, 256, 1024], F32, kind="ExternalOutput").ap()
    if dbg:
        xT_d = nc.dram_tensor("xT_dbg", [16, 128, NTOK], F32, kind="ExternalOutput").ap()
        hT_dbg = nc.dram_tensor("hT_dbg", [128, 16 * NTOK], BF16, kind="ExternalOutput").ap()
        ar_dbg = nc.dram_tensor("ar_dbg", [128, 23424], F32, kind="ExternalOutput").ap()
        cols_dbg = nc.dram_tensor("cols_dbg", [128, 1400], F32, kind="ExternalOutput").ap()
    else:
        xT_d = nc.dram_tensor("xT_scratch", [16, 128, NTOK], F32).ap()
    xT_v = xT_d.rearrange("j p t -> p j t")

    hT = p.sbuf("hT", [128, 16, NTOK], BF16)
    ring = [p.sbuf(f"ring{i}", [128, 8192], BF16) for i in range(3)]
    cosT = p.sbuf("cosT", [128, 1024], F32)
    sinT = p.sbuf("sinT", [128, 1024], F32)
    identf = p.sbuf("identf", [128, 128], F32)
    identb = p.sbuf("identb", [128, 128], BF16)
    onesf = p.sbuf("onesf", [128, 128], F32)
    permf = p.sbuf("permf", [128, 128], F32)
    NCOLS = 1400
    cols = p.sbuf("cols", [128, NCOLS], F32)
    s_bf = p.sbuf("s_bf", [128, 32], BF16)
    lamt = p.sbuf("lamt", [128, 2 * 8], F32)
    lamraw = p.sbuf("lamraw", [128, 2 * 256], F32)
    gsub = p.sbuf("gsub", [128, 2 * 128], F32)
    ARENA = 23424
    arena = p.sbuf("arena", [128, ARENA], F32)
    ps = p.psum("ps", [128, 8, 512], F32)

    def A32(off_b, n):
        assert off_b % 4 == 0 and off_b // 4 + n <= ARENA, (off_b, n)
        return arena[:, off_b // 4: off_b // 4 + n]

    def A16(off_b, n):
        assert off_b % 4 == 0 and n % 2 == 0 and off_b // 4 + n // 2 <= ARENA, (off_b, n)
        return arena[:, off_b // 4: off_b // 4 + n // 2].bitcast(BF16)

    colmap = {}
    coff = [0]

    def calloc(name, n):
        colmap[name] = coff[0]
        coff[0] += n
        assert coff[0] <= NCOLS
        return colmap[name]

    def C(name, i=0, n=1):
        o = colmap[name] + i
        return cols[:, o:o + n]

    S_W = [p.dsem(f"s_w{i}") for i in range(3)]
    S_L = [p.dsem(f"s_l{i}") for i in range(4)]
    S_S = [p.dsem(f"s_s{i}") for i in range(4)]
    S_C = p.dsem("s_c")
    S_P = p.dsem("s_p")
    ring_i = [0]

    pre_cache = {}

    def wload(pieces, queue="pool", key=None):
        if key is not None and key in pre_cache:
            return pre_cache.pop(key)
        si = ring_i[0] % 3
        ring_i[0] += 1
        pairs = []
        for off, src, kc, n in pieces:
            dst = ring[si][:, off:off + kc * n].rearrange("p (k n) -> p k n", k=kc)
            pairs.append((dst, src.rearrange("(k p) n -> p k n", p=128)))
        p.dma(queue, pairs, S_W[si], w=[("ring", si)])
        return si

    def prefetch(key, pieces):
        pre_cache[key] = wload(pieces)

    def rview(si, off, kc, n):
        return ring[si][:, off:off + kc * n].rearrange("p (k n) -> p k n", k=kc)

    ps_rr = [0]

    def psbank(lo=0, hi=8):
        b = lo + ps_rr[0] % (hi - lo)
        ps_rr[0] += 1
        return b

    ld_rr = [0]

    def lsem():
        ld_rr[0] += 1
        return S_L[ld_rr[0] % 4]

    st_rr = [0]

    def ssem():
        st_rr[0] += 1
        return S_S[st_rr[0] % 4]

    if dbg:
        p.op("dve", lambda e: e.memset(arena[:], 0.0), w=["arena0"])
        p.op("dve", lambda e: e.memset(cols[:], 0.0), w=["cols0"])
        p.op("pool", lambda e: e.memset(hT[:], 0.0), w=["hT0"])
        p.barrier()
    p.dma("sp", [(identf[:], c_ident)], S_C, w=["identf"])
    p.dma("sp", [(permf[:], c_perm)], lsem(), w=["permf"])
    p.dma("sp", [(cosT[:], c_cos)], lsem(), w=["cos"])
    p.dma("sp", [(sinT[:], c_sin)], lsem(), w=["sin"])
    p.op("dve", lambda e: e.memset(onesf[:], 1.0), w=["ones"])
    p.op("dve", lambda e: e.tensor_copy(out=identb[:], in_=identf[:]), r=["identf"], w=["identb"])
    for l in range(DEPTH):
        p.dma("sp", [(lamraw[:, l * 256:(l + 1) * 256], da_lambda[l:l + 1, :].partition_broadcast(128))], lsem(),
              w=[("lamraw", l)])
        p.dma("sp", [(gsub[:, l * 128:(l + 1) * 128], da_subln[l:l + 1, :].partition_broadcast(128))], lsem(),
              w=[("gsub", l)])

    stg_i = [0]

    def to_cols(name, src_rows, R):
        off = calloc(name, R)
        k = stg_i[0] % 4
        stg_i[0] += 1
        stg = A32(k * 512, 128)
        bank = psbank()
        p.dma("sp", [(stg[0:R, :], src_rows)], lsem(), w=[("stg", k)])
        p.mm([lambda e: e.transpose(out=ps[:, bank, 0:R], in_=stg[0:R, :], identity=identf[0:R, 0:R])],
             r=[("stg", k), "identf"], w=[("ps", bank)])
        p.op("dve", lambda e: e.tensor_copy(out=cols[:, off:off + R], in_=ps[:, bank, 0:R]),
             r=[("ps", bank)], w=[("col", name)])

    calloc("eps", 1)
    p.op("dve", lambda e: e.memset(C("eps"), EPS), w=[("col", "eps")])
    to_cols("cvec", cvec.rearrange("v (k p) -> (v k) p", p=128), 32)
    to_cols("gfin", g_final.rearrange("o (k p) -> (o k) p", p=128), 16)
    for l in range(DEPTH):
        to_cols(f"g1_{l}", g_norm1[l:l + 1, :].rearrange("o (k p) -> (o k) p", p=128), 16)
        to_cols(f"g2_{l}", g_norm2[l:l + 1, :].rearrange("o (k p) -> (o k) p", p=128), 16)
        to_cols(f"bg_{l}", b_gate[l:l + 1, :].rearrange("o (k p) -> (o k) p", p=128), 48)
        to_cols(f"scw_{l}", sc_conv[l].rearrange("t (k p) -> (t k) p", p=128), 12)
        to_cols(f"cfw_{l}", cf_conv[l].rearrange("t (k p) -> (t k) p", p=128), 124)
        to_cols(f"cfb_{l}", cf_conv_b[l:l + 1, :].rearrange("o (k p) -> (o k) p", p=128), 4)
        to_cols(f"lng_{l}", cf_ln_g[l:l + 1, :].rearrange("o (k p) -> (o k) p", p=128), 4)
        to_cols(f"lnb_{l}", cf_ln_b[l:l + 1, :].rearrange("o (k p) -> (o k) p", p=128), 4)
        to_cols(f"bmod_{l}", b_mod[l:l + 1, :].rearrange("o (k p) -> (o k) p", p=128), 96)
        for v in range(2):
            calloc(f"mod_{l}_{v}", 96)
            calloc(f"A1_{l}_{v}", 16)
            calloc(f"A2_{l}_{v}", 16)
    p.op("act", lambda e: e.activation(out=s_bf[:], in_=C("cvec", 0, 32), func=AF.Silu), r=[("col", "cvec")], w=["s_bf"])
    for l in range(DEPTH):
        lam_init = 0.8 - 0.6 * math.exp(-0.3 * l)
        lr = lamraw[:, l * 256:(l + 1) * 256]
        lt = lamt[:, l * 8:(l + 1) * 8]
        k0 = ("lamt", l)
        p.op("dve", lambda e, lr=lr: e.tensor_tensor(out=lr[:, 0:64], in0=lr[:, 0:64], in1=lr[:, 64:128], op=ALU.mult),
             r=[("lamraw", l)], w=[("lamraw", l)])
        p.op("dve", lambda e, lr=lr: e.tensor_tensor(out=lr[:, 128:192], in0=lr[:, 128:192], in1=lr[:, 192:256], op=ALU.mult),
             r=[("lamraw", l)], w=[("lamraw", l)])
        p.op("dve", lambda e, lr=lr, lt=lt: e.reduce_sum(out=lt[:, 0:1], in_=lr[:, 0:64], axis=mybir.AxisListType.X),
             r=[("lamraw", l)], w=[k0])
        p.op("dve", lambda e, lr=lr, lt=lt: e.reduce_sum(out=lt[:, 1:2], in_=lr[:, 128:192], axis=mybir.AxisListType.X),
             r=[("lamraw", l)], w=[k0])
        p.op("act", lambda e, lt=lt: e.activation(out=lt[:, 2:4], in_=lt[:, 0:2], func=AF.Exp), r=[k0], w=[k0])
        p.op("dve", lambda e, lt=lt, li=lam_init: e.scalar_tensor_tensor(out=lt[:, 4:5], in0=lt[:, 3:4], scalar=-li, in1=lt[:, 2:3],
                                                                       op0=ALU.add, op1=ALU.subtract), r=[k0], w=[k0])
        p.op("dve", lambda e, l=l, li=lam_init: e.tensor_scalar(out=gsub[:, l * 128:(l + 1) * 128], in0=gsub[:, l * 128:(l + 1) * 128],
                                                               scalar1=1.0 - li, scalar2=None, op0=ALU.mult),
             r=[("gsub", l)], w=[("gsub", l)])

    def x0_gen():
        for t in range(12):
            xb = A32(8192 + (t % 2) * 8192, 2048)
            sg = A32(8192 + 16384 + (t % 2) * 8192, 2048)
            p.dma("sp", [(xb, xin[t * 128:(t + 1) * 128, :])], lsem(), w=[("x0b", t % 2)])
            for half in range(2):
                b0 = 4 * half
                fns = []
                for jj in range(8):
                    j = (0 if half == 0 else 8) + jj
                    fns.append(lambda e, j=j, jj=jj, b0=b0, xb=xb: e.transpose(
                        out=ps[:, b0 + jj // 4, (jj % 4) * 128:(jj % 4 + 1) * 128], in_=xb[:, j * 128:(j + 1) * 128], identity=identf[:]))
                p.mm(fns, r=[("x0b", t % 2), "identf"], w=[("ps", b0), ("ps", b0 + 1)])
                eng = "act" if half == 0 else "dve"
                if eng == "act":
                    p.op("act", lambda e, b0=b0, sg=sg, half=half: e.activation(
                        out=sg[:, half * 1024:(half + 1) * 1024].rearrange("p (a b) -> p a b", a=2), in_=ps[:, b0:b0 + 2, :], func=AF.Copy),
                        r=[("ps", b0), ("ps", b0 + 1)], w=[("x0s", t % 2, half)])
                else:
                    p.op("dve", lambda e, b0=b0, sg=sg, half=half: e.tensor_copy(
                        out=sg[:, half * 1024:(half + 1) * 1024].rearrange("p (a b) -> p a b", a=2), in_=ps[:, b0:b0 + 2, :]),
                        r=[("ps", b0), ("ps", b0 + 1)], w=[("x0s", t % 2, half)])
            p.dma("sp", [(xT_v[:, :, t * 128:(t + 1) * 128], sg.rearrange("p (j c) -> p j c", j=16))], ssem(),
                  r=[("x0s", t % 2, 0), ("x0s", t % 2, 1)], w=[("xT0", t)])
            yield

    def mod_pieces(l, cb):
        return [(0, w_mod[l][:, cb * 512:(cb + 1) * 512], 16, 512)]

    MODBANK = 7

    def mod_load(l, cb):
        return wload(mod_pieces(l, cb), key=("mod", l, cb))

    def mod_mm(l, cb, si, bank=7):
        sview = s_bf[:].rearrange("p (v k) -> p v k", v=2)
        wv = rview(si, 0, 16, 512)
        for c in range(4):
            fns = [lambda e, kc=kc, c=c, wv=wv: e.matmul(ps[:, bank, c * 2:c * 2 + 2], lhsT=wv[:, kc, c * 128:(c + 1) * 128],
                                                          rhs=sview[:, :, kc], start=(kc == 0), stop=(kc == 15))
                   for kc in range(16)]
            p.mm(fns, r=[("ring", si), "s_bf"], w=[("ps", bank)])
        pv = ps[:, bank, 0:8].rearrange("p (q v) -> p q v", v=2)
        q0 = cb * 4
        for v in range(2):
            p.op("dve", lambda e, v=v: e.tensor_tensor(out=C(f"mod_{l}_{v}", q0, 4), in0=pv[:, :, v], in1=C(f"bmod_{l}", q0, 4), op=ALU.add),
                 r=[("ps", bank), ("col", f"bmod_{l}")], w=[("col", f"mod_{l}_{v}")])
        if cb == 7:
            mod_derive(l, 1)
        if cb == 19:
            mod_derive(l, 2)

    def mod_slab(l, cb, bank=7):
        mod_mm(l, cb, mod_load(l, cb), bank)

    def mod_derive(l, which):
        for v in range(2):
            mk = ("col", f"mod_{l}_{v}")
            if which == 1:
                p.op("dve", lambda e, v=v: e.scalar_tensor_tensor(out=C(f"A1_{l}_{v}", 0, 16), in0=C(f"mod_{l}_{v}", 16, 16), scalar=1.0,
                                                                 in1=C(f"g1_{l}", 0, 16), op0=ALU.add, op1=ALU.mult),
                     r=[mk, ("col", f"g1_{l}")], w=[("col", f"A1_{l}_{v}")])
            else:
                p.op("dve", lambda e, v=v: e.scalar_tensor_tensor(out=C(f"A2_{l}_{v}", 0, 16), in0=C(f"mod_{l}_{v}", 64, 16), scalar=1.0,
                                                                 in1=C(f"g2_{l}", 0, 16), op0=ALU.add, op1=ALU.mult),
                     r=[mk, ("col", f"g2_{l}")], w=[("col", f"A2_{l}_{v}")])

    MOD_PRE = {0: list(range(8, 16)), 1: list(range(12, 18))}
    MOD_ATT = {0: list(range(16, 24)), 1: list(range(18, 24))}

    def phase_pre(l):
        if l == 0:
            xg = x0_gen()
            for t in range(12):
                next(xg)
                if t < 8:
                    mod_slab(l, t)
            p.barrier()
        todo = list(MOD_PRE[l])
        steps = 0
        per = max(1, 56 // max(1, len(todo)))
        for _ in norm_gen(l, 1):
            steps += 1
            if steps % per == 0 and todo:
                mod_slab(l, todo.pop(0))
        while todo:
            mod_slab(l, todo.pop(0))

    def phase_n2(l):
        if l + 1 >= DEPTH:
            phase_norm(l, 2)
            return
        todo = list(range(12))
        steps = 0
        for _ in norm_gen(l, 2):
            steps += 1
            if steps % 4 == 0 and todo:
                mod_slab(l + 1, todo.pop(0))
        while todo:
            mod_slab(l + 1, todo.pop(0))

    def rstd_from_psum(bank, n, dst, scale, key):
        p.op("dve", lambda e: e.tensor_scalar(out=dst, in0=ps[:, bank, 0:n], scalar1=scale, scalar2=EPS, op0=ALU.mult, op1=ALU.add),
             r=[("ps", bank)], w=[key])
        p.op("act", lambda e: e.activation(out=dst, in_=dst, func=AF.Sqrt), r=[key], w=[key])
        p.op("dve", lambda e: e.reciprocal(out=dst, in_=dst), r=[key], w=[key])

    def phase_norm(l, which):
        for _ in norm_gen(l, which):
            pass

    def norm_gen(l, which):
        for tb in range(3):
            v = 0 if tb == 0 else 1
            An = f"A{which}_{l}_{v}"
            Bc = (f"mod_{l}_{v}", 0 if which == 1 else 48)
            xb = A32((tb % 2) * 32768, 8192).rearrange("p (j t) -> p j t", j=16)
            kx = ("nx", tb % 2)
            p.dma("sp", [(xb, xT_v[:, :, tb * 512:(tb + 1) * 512])], lsem(), w=[kx])
            bank = psbank(0, 2)
            for jg in range(4):
                sq = A32(65536 + (jg % 2) * 8192, 2048)
                ksq = ("nsq", jg % 2)
                p.op("act", lambda e, jg=jg, sq=sq, xb=xb: e.activation(out=sq, in_=xb[:, 4 * jg:4 * jg + 4, :].rearrange("p j t -> p (j t)"), func=AF.Square),
                     r=[kx], w=[ksq])
                p.mm([lambda e, jg=jg, k=k, sq=sq, bank=bank: e.matmul(ps[:, bank, :], lhsT=onesf[:], rhs=sq[:, k * 512:(k + 1) * 512],
                                                                      start=(jg == 0 and k == 0), stop=(jg == 3 and k == 3)) for k in range(4)],
                     r=[ksq, "ones"], w=[("ps", bank)])
                yield
            rs = A32(65536 + 16384, 512)
            rstd_from_psum(bank, 512, rs, 1.0 / D, "nrs")
            for j in range(16):
                tm = A32(65536 + 18432 + (j % 2) * 2048, 512)
                p.op("dve", lambda e, j=j, tm=tm, xb=xb: e.tensor_tensor(out=tm, in0=xb[:, j, :], in1=rs, op=ALU.mult),
                     r=[kx, "nrs"], w=[("ntm", j % 2)])
                p.op("act", lambda e, j=j, tm=tm, An=An, Bc=Bc, tb=tb: e.activation(
                    out=hT[:, j, tb * 512:(tb + 1) * 512], in_=tm, func=AF.Identity, scale=C(An, j), bias=C(Bc[0], Bc[1] + j)),
                    r=[("ntm", j % 2), ("col", An), ("col", Bc[0])], w=[("hT", tb)])
                yield

    def proj(bank, wv, c0, M, src, t0, n, kcs, rkeys):
        fns = [lambda e, i=i, kc=kc: e.matmul(ps[0:M, bank, 0:n], lhsT=wv[:, i, c0:c0 + M], rhs=src[:, kc, t0:t0 + n],
                                              start=(i == 0), stop=(i == len(kcs) - 1)) for i, kc in enumerate(kcs)]
        return p.mm(fns, r=rkeys, w=[("ps", bank)])

    AT_OFF = 0
    YB_OFF = 24576
    YC_OFF = 36864
    PH_OFF = 49152

    def attn_pieces(l, h):
        return [(0, w_in[l][:, h * 128:(h + 1) * 128], 16, 128),
                (2048, w_in[l][:, OFF_K + h * 128:OFF_K + (h + 1) * 128], 16, 128),
                (4096, w_in[l][:, OFF_V + h * 128:OFF_V + (h + 1) * 128], 16, 128)]

    def phase_attn(l):
        attnT = A16(AT_OFF, 8 * NTOK).rearrange("p (h t) -> p h t", h=8)
        o = 24576
        qTs, kTs, Vas = [], [], []
        for s_ in range(2):
            qTs.append(A16(o, NTOK)); o += NTOK * 2
            kTs.append(A16(o, 1792)); o += 1792 * 2
            Vas.append(A16(o, 14 * 130).rearrange("p (t e) -> p t e", t=14)); o += 14 * 130 * 2
        PT = A16(o, 10 * 2 * 512).rearrange("p (k m q) -> p k m q", k=10, m=2); o += 20480
        qf = [A32(o + i * 2048, 512) for i in range(2)]; o += 4096
        t1 = A32(o, 512); o += 2048
        t2 = A32(o, 512); o += 2048
        ckb = A16(o, 128); o += 256
        stg = [A32(o + i * 512, 128) for i in range(2)]; o += 1024
        osa = A32(o, 12 * 128).rearrange("p (t e) -> p t e", t=12); o += 6144
        obf = [A16(o + i * 256, 128) for i in range(2)]; o += 512
        rr = A32(o, 4 * 12).rearrange("p (a t) -> p a t", a=4); o += 192
        sqj = A32(o, 128); o += 512
        vtb = [A16(o + i * 1024, 512) for i in range(2)]; o += 2048
        assert o <= ARENA * 4, o
        lt = lamt[:, l * 8:(l + 1) * 8]
        gs = gsub[:, l * 128:(l + 1) * 128]
        HK = [("hT", 0), ("hT", 1), ("hT", 2)]
        pbank = [0]

        def pb_next():
            pbank[0] += 1
            return pbank[0] % 2

        def proj_gen(h, S):
            qT, kT, Va = qTs[S], kTs[S], Vas[S]
            si = wload(attn_pieces(l, h), key=("attn", l, h))
            yield
            wq, wk, wvv = rview(si, 0, 16, 128), rview(si, 2048, 16, 128), rview(si, 4096, 16, 128)
            p.op("pool", lambda e: e.memset(Va[:, :, 128:130], 1.0), w=[("Va", S, t) for t in range(14)])
            for kind in range(2):
                wv = wq if kind == 0 else wk
                dstT = qT if kind == 0 else kT
                dkey = "qT" if kind == 0 else "kT"
                sc = 0.125 if kind == 0 else 1.0
                for tb in range(3):
                    bank = pb_next()
                    proj(bank, wv, 0, 128, hT, tb * 512, 512, list(range(16)), [("ring", si), HK[tb]])
                    if tb == 0:
                        p.op("act", lambda e, bank=bank, dstT=dstT, sc=sc: e.activation(
                            out=dstT[:, 0:512], in_=ps[:, bank, :], func=AF.Copy, scale=sc),
                            r=[("ps", bank)], w=[(dkey, S, 0)])
                    else:
                        qq = qf[tb % 2]
                        p.op("act", lambda e, bank=bank, qq=qq, sc=sc: e.activation(out=qq, in_=ps[:, bank, :], func=AF.Copy, scale=sc),
                             r=[("ps", bank)], w=[("qf", tb % 2)])
                        pb = pb_next()
                        p.mm([lambda e, qq=qq, pb=pb: e.matmul(ps[:, pb, :], lhsT=permf[:], rhs=qq, start=True, stop=True)],
                             r=[("qf", tb % 2), "permf"], w=[("ps", pb)])
                        cs = slice((tb - 1) * 512, tb * 512)
                        p.op("dve", lambda e, qq=qq, cs=cs: e.tensor_tensor(out=t1, in0=qq, in1=cosT[:, cs], op=ALU.mult),
                             r=[("qf", tb % 2), "cos"], w=["t1"])
                        p.op("dve", lambda e, cs=cs, pb=pb: e.tensor_tensor(out=t2, in0=ps[:, pb, :], in1=sinT[:, cs], op=ALU.mult),
                             r=[("ps", pb), "sin"], w=["t2"])
                        p.op("dve", lambda e, tb=tb, dstT=dstT: e.tensor_tensor(out=dstT[:, tb * 512:(tb + 1) * 512], in0=t1, in1=t2, op=ALU.add),
                             r=["t1", "t2"], w=[(dkey, S, tb)])
                    yield
                if kind == 1:
                    for t in range(4):
                        bank = pb_next()
                        fns = [lambda e, kc=kc, t=t, bank=bank, wv=wv: e.matmul(ps[:, bank, 0:128], lhsT=hT[:, kc, t * 128:(t + 1) * 128], rhs=wv[:, kc, :],
                                                                                start=(kc == 0), stop=(kc == 15)) for kc in range(16)]
                        p.mm(fns, r=[("ring", si), HK[0]], w=[("ps", bank)])
                        sg = stg[t % 2]
                        p.op("act", lambda e, bank=bank, sg=sg: e.activation(out=sg, in_=ps[:, bank, 0:128], func=AF.Copy),
                             r=[("ps", bank)], w=[("stg", t % 2)])
                        p.dma("sp", [(nk_out[t // 2, l, (t % 2) * 128:(t % 2 + 1) * 128, h * 128:(h + 1) * 128], sg)], ssem(),
                              r=[("stg", t % 2)], w=[("nk", t, h)])
                        yield
                    for t in range(2):
                        p.dma("pool", [(ckb, ck[l, t * 128:(t + 1) * 128, h * 128:(h + 1) * 128])], S_P, w=["ckb"])
                        bank = pb_next()
                        pbv = ps[:, bank, 0:64].bitcast(BF16)
                        p.mm([lambda e, pbv=pbv: e.transpose(out=pbv, in_=ckb, identity=identb[:])], r=["ckb", "identb"], w=[("ps", bank)])
                        p.op("dve", lambda e, pbv=pbv, t=t: e.tensor_copy(out=kT[:, 1536 + t * 128:1536 + (t + 1) * 128], in_=pbv),
                             r=[("ps", bank)], w=[("kT", S, 3 + t)])
                    yield
            for t in range(4):
                bank = pb_next()
                fns = [lambda e, kc=kc, t=t, bank=bank: e.matmul(ps[:, bank, 0:128], lhsT=hT[:, kc, t * 128:(t + 1) * 128], rhs=wvv[:, kc, :],
                                                                 start=(kc == 0), stop=(kc == 15)) for kc in range(16)]
                p.mm(fns, r=[("ring", si), HK[0]], w=[("ps", bank)])
                sg = stg[t % 2]
                p.op("act", lambda e, bank=bank, sg=sg: e.activation(out=sg, in_=ps[:, bank, 0:128], func=AF.Copy),
                     r=[("ps", bank)], w=[("stg", t % 2)])
                p.dma("sp", [(nv_out[t // 2, l, (t % 2) * 128:(t % 2 + 1) * 128, h * 128:(h + 1) * 128], sg)], ssem(),
                      r=[("stg", t % 2)], w=[("nv", t, h)])
                p.op("dve", lambda e, sg=sg, t=t: e.tensor_copy(out=Va[:, t, 0:128], in_=sg), r=[("stg", t % 2)], w=[("Va", S, t)])
                yield
            for tb in (1, 2):
                bank = pb_next()
                proj(bank, wvv, 0, 128, hT, tb * 512, 512, list(range(16)), [("ring", si), HK[tb]])
                vb = vtb[tb % 2]
                p.op("act", lambda e, bank=bank, vb=vb: e.activation(out=vb, in_=ps[:, bank, :], func=AF.Copy), r=[("ps", bank)], w=[("vtb", tb % 2)])
                bank2 = pb_next()
                pbv = ps[:, bank2, 0:256].bitcast(BF16)
                p.mm([lambda e, i=i, pbv=pbv, vb=vb: e.transpose(out=pbv[:, i * 128:(i + 1) * 128], in_=vb[:, i * 128:(i + 1) * 128], identity=identb[:])
                      for i in range(4)], r=[("vtb", tb % 2), "identb"], w=[("ps", bank2)])
                p.op("dve", lambda e, pbv=pbv, tb=tb: e.tensor_copy(out=Va[:, 4 * tb:4 * tb + 4, 0:128], in_=pbv.rearrange("p (i e) -> p i e", i=4)),
                     r=[("ps", bank2)], w=[("Va", S, t) for t in range(4 * tb, 4 * tb + 4)])
                yield
            for t in range(2):
                p.dma("pool", [(Va[:, 12 + t, 0:128], cv[l, t * 128:(t + 1) * 128, h * 128:(h + 1) * 128])], S_P, w=[("Va", S, 12 + t)])
            yield

        stb = [0]

        def attn_gen(h, S):
            qT, kT, Va = qTs[S], kTs[S], Vas[S]
            blocks = [(0, 256, [0, 1]), (256, 256, [2, 3]), (512, 512, list(range(4, 14))), (1024, 512, list(range(4, 14)))]
            for (qs, nq, ktiles) in blocks:
                nk_ = len(ktiles)
                for ki, kt in enumerate(ktiles):
                    kcol = kt * 128 if kt < 12 else 1536 + (kt - 12) * 128
                    ktb = kt // 4 if kt < 12 else 3 + (kt - 12)
                    stb[0] += 1
                    bank = 2 + 2 * (stb[0] % 2)
                    fns = [lambda e, m=m, bank=bank, kcol=kcol, qs=qs, nq=nq: e.matmul(
                        ps[:, bank + m, 0:nq], lhsT=kT[m * 64:(m + 1) * 64, kcol:kcol + 128], rhs=qT[m * 64:(m + 1) * 64, qs:qs + nq],
                        start=True, stop=True) for m in range(2)]
                    p.mm(fns, r=[("kT", S, ktb)] + [("qT", S, tb_) for tb_ in range(qs // 512, (qs + nq - 1) // 512 + 1)],
                         w=[("ps", bank), ("ps", bank + 1)])
                    for m in range(2):
                        p.op("act", lambda e, ki=ki, bank=bank, m=m, nq=nq: e.activation(out=PT[:, ki, m, 0:nq], in_=ps[:, bank + m, 0:nq], func=AF.Exp),
                             r=[("ps", bank + m)], w=[("PT", ki, m)])
                    yield
                for qt in range(nq // 128):
                    tile = (qs + qt * 128) // 128
                    bank = 6 + qt % 2
                    ov = ps[:, bank, 0:260].rearrange("p (m e) -> p m e", m=2)
                    fns = []
                    for m in range(2):
                        for ki, kt in enumerate(ktiles):
                            fns.append(lambda e, m=m, ki=ki, kt=kt, qt=qt, ov=ov, nk_=nk_: e.matmul(
                                ov[:, m, 0:129], lhsT=PT[:, ki, m, qt * 128:(qt + 1) * 128], rhs=Va[:, kt, 0:129],
                                start=(ki == 0), stop=(ki == nk_ - 1)))
                    p.mm(fns, r=[("PT", ki, m) for ki in range(nk_) for m in range(2)] + [("Va", S, kt) for kt in ktiles], w=[("ps", bank)])
                    kr = ("rr", tile)
                    p.op("dve", lambda e, ov=ov, tile=tile: e.reciprocal(out=rr[:, 0:2, tile], in_=ov[:, :, 128]), r=[("ps", bank)], w=[kr])
                    p.op("dve", lambda e, ov=ov, tile=tile: e.tensor_scalar(out=osa[:, tile, :], in0=ov[:, 1, 0:128], scalar1=rr[:, 1, tile:tile + 1],
                                                                          scalar2=lt[:, 4:5], op0=ALU.mult, op1=ALU.mult),
                         r=[("ps", bank), kr, ("lamt", l)], w=[("osa", tile)])
                    p.op("dve", lambda e, ov=ov, tile=tile: e.scalar_tensor_tensor(out=osa[:, tile, :], in0=ov[:, 0, 0:128], scalar=rr[:, 0, tile:tile + 1],
                                                                                 in1=osa[:, tile, :], op0=ALU.mult, op1=ALU.add),
                         r=[("ps", bank), kr, ("osa", tile)], w=[("osa", tile)])
                    p.op("dve", lambda e, tile=tile: e.scalar_tensor_tensor(out=sqj, in0=osa[:, tile, :], scalar=1.0, in1=osa[:, tile, :],
                                                                          op0=ALU.mult, op1=ALU.mult, accum_out=rr[:, 2, tile:tile + 1]),
                         r=[("osa", tile)], w=[("ss", tile), "sqj"])
                    yield
            p.op("act", lambda e: e.activation(out=rr[:, 3, :], in_=rr[:, 2, :], func=AF.Sqrt, scale=1.0 / 128, bias=C("eps")),
                 r=[("ss", t) for t in range(12)] + [("col", "eps")], w=["rstd"])
            p.op("dve", lambda e: e.reciprocal(out=rr[:, 3, :], in_=rr[:, 3, :]), r=["rstd"], w=["rstd"])
            for tile in range(12):
                oi = tile % 2
                p.op("dve", lambda e, oi=oi, tile=tile: e.scalar_tensor_tensor(out=obf[oi], in0=osa[:, tile, :], scalar=rr[:, 3, tile:tile + 1], in1=gs,
                                                                             op0=ALU.mult, op1=ALU.mult),
                     r=[("osa", tile), "rstd", ("gsub", l)], w=[("obf", oi)])
                tbank = 6 + oi
                tpv = ps[:, tbank, 384:448].bitcast(BF16)
                p.mm([lambda e, oi=oi, tpv=tpv: e.transpose(out=tpv, in_=obf[oi], identity=identb[:])],
                     r=[("obf", oi), "identb"], w=[("ps", tbank)])
                p.op("act", lambda e, h=h, tile=tile, tpv=tpv: e.activation(out=attnT[:, h, tile * 128:(tile + 1) * 128], in_=tpv, func=AF.Copy),
                     r=[("ps", tbank)], w=[("attnT", h)])
            yield

        for _ in proj_gen(0, 0):
            pass
        mtodo = list(MOD_ATT[l])
        for h in range(8):
            ag = attn_gen(h, h % 2)
            pg = proj_gen(h + 1, (h + 1) % 2) if h + 1 < 8 else None
            a_alive, p_alive = True, pg is not None
            mpend = None
            if p_alive:
                next(pg)
            if mtodo:
                mcb = mtodo.pop(0)
                mpend = (mcb, mod_load(l, mcb))
            while a_alive or p_alive:
                for _ in range(AT_RATIO[0]):
                    if a_alive:
                        try:
                            next(ag)
                        except StopIteration:
                            a_alive = False
                if p_alive:
                    try:
                        next(pg)
                    except StopIteration:
                        p_alive = False
            if mpend is not None:
                mod_mm(l, mpend[0], mpend[1], bank=pb_next())

    def zpos(tok):
        return 1 + tok if tok < 256 else (258 + (tok - 256) if tok < 512 else 515 + (tok - 512))

    def sc_pieces(l, i):
        return [(0, w_in[l][:, OFF_SC + i * 512:OFF_SC + (i + 1) * 512], 16, 512)]

    def phase_sc(l):
        ybT = A16(YB_OFF, 4 * NTOK).rearrange("p (c t) -> p c t", c=4)
        o = YC_OFF
        zp = [A32(o + c * 6160, 1540) for c in range(4)]; o += 4 * 6160
        yc = [A32(o + c * 6160, 1540) for c in range(4)]; o += 4 * 6160
        gcs = [A32(o + i * 2048, 512) for i in range(2)]; o += 4096
        assert o <= ARENA * 4, o
        HK = [("hT", 0), ("hT", 1), ("hT", 2)]
        for c in range(4):
            p.op("pool", lambda e, c=c: e.memset(zp[c], 0.0), w=[("zp", c)])
        s_gc = wload(sc_pieces(l, 1), key=("sc", l, 1))
        s_sx = wload(sc_pieces(l, 2), key=("sc", l, 2))
        wgc = rview(s_gc, 0, 16, 512)
        wsx = rview(s_sx, 0, 16, 512)
        for c in range(4):
            for tb in range(3):
                b1 = psbank(0, 4)
                proj(b1, wgc, c * 128, 128, hT, tb * 512, 512, list(range(16)), [("ring", s_gc), HK[tb]])
                g = gcs[tb % 2]
                p.op("act", lambda e, b1=b1, g=g: e.activation(out=g, in_=ps[:, b1, :], func=AF.Copy), r=[("ps", b1)], w=[("gcs", tb % 2)])
                b2 = psbank(0, 4)
                proj(b2, wsx, c * 128, 128, hT, tb * 512, 512, list(range(16)), [("ring", s_sx), HK[tb]])
                if tb == 0:
                    for s_ in range(2):
                        z0 = zpos(s_ * 256)
                        p.op("dve", lambda e, b2=b2, g=g, c=c, s_=s_, z0=z0: e.tensor_tensor(
                            out=zp[c][:, z0:z0 + 256], in0=ps[:, b2, s_ * 256:(s_ + 1) * 256], in1=g[:, s_ * 256:(s_ + 1) * 256], op=ALU.mult),
                            r=[("ps", b2), ("gcs", tb % 2)], w=[("zp", c)])
                else:
                    z0 = zpos(tb * 512)
                    p.op("dve", lambda e, b2=b2, g=g, c=c, z0=z0: e.tensor_tensor(out=zp[c][:, z0:z0 + 512], in0=ps[:, b2, :], in1=g, op=ALU.mult),
                         r=[("ps", b2), ("gcs", tb % 2)], w=[("zp", c)])
            L = 1538
            wc = f"scw_{l}"
            p.op("dve", lambda e, c=c: e.tensor_scalar(out=yc[c][:, 1:1 + L], in0=zp[c][:, 1:1 + L], scalar1=C(wc, 4 + c), scalar2=None, op0=ALU.mult),
                 r=[("zp", c), ("col", wc)], w=[("yc", c)])
            p.op("dve", lambda e, c=c: e.scalar_tensor_tensor(out=yc[c][:, 1:1 + L], in0=zp[c][:, 0:L], scalar=C(wc, 0 + c), in1=yc[c][:, 1:1 + L],
                                                             op0=ALU.mult, op1=ALU.add), r=[("zp", c), ("yc", c), ("col", wc)], w=[("yc", c)])
            p.op("dve", lambda e, c=c: e.scalar_tensor_tensor(out=yc[c][:, 1:1 + L], in0=zp[c][:, 2:2 + L], scalar=C(wc, 8 + c), in1=yc[c][:, 1:1 + L],
                                                             op0=ALU.mult, op1=ALU.add), r=[("zp", c), ("yc", c), ("col", wc)], w=[("yc", c)])
        s_gb = wload([(0, w_in[l][:, OFF_SC:OFF_SC + 512], 16, 512)])
        wgb = rview(s_gb, 0, 16, 512)
        for c in range(4):
            for tb in range(3):
                b1 = psbank(0, 4)
                proj(b1, wgb, c * 128, 128, hT, tb * 512, 512, list(range(16)), [("ring", s_gb), HK[tb]])
                if tb == 0:
                    for s_ in range(2):
                        z0 = zpos(s_ * 256)
                        p.op("dve", lambda e, b1=b1, c=c, s_=s_, z0=z0: e.tensor_tensor(
                            out=ybT[:, c, s_ * 256:(s_ + 1) * 256], in0=ps[:, b1, s_ * 256:(s_ + 1) * 256], in1=yc[c][:, z0:z0 + 256], op=ALU.mult),
                            r=[("ps", b1), ("yc", c)], w=[("ybT", c)])
                else:
                    z0 = zpos(tb * 512)
                    p.op("dve", lambda e, b1=b1, c=c, tb=tb, z0=z0: e.tensor_tensor(
                        out=ybT[:, c, tb * 512:(tb + 1) * 512], in0=ps[:, b1, :], in1=yc[c][:, z0:z0 + 512], op=ALU.mult),
                        r=[("ps", b1), ("yc", c)], w=[("ybT", c)])

    def cpos(tok):
        return 15 + tok if tok < 256 else (286 + (tok - 256) if tok < 512 else 557 + (tok - 512))

    def cf_pieces(l, i):
        return [(0, w_in[l][:, OFF_CF + i * 512:OFF_CF + (i + 1) * 512], 16, 512)]

    def phase_cf(l):
        ycT = A16(YC_OFF, 4 * NTOK).rearrange("p (c t) -> p c t", c=4)
        dg = A16(YC_OFF, 31 * 128).rearrange("p (j m) -> p j m", j=31)
        o = PH_OFF
        zb = [A16(o + i * 3200, 1596) for i in range(2)]; o += 6400
        cy = A32(o, 4 * NTOK).rearrange("p (c t) -> p c t", c=4); o += 4 * NTOK * 4
        sg = [A32(o + i * 2048, 512) for i in range(2)]; o += 4096
        mean = A32(o, 512); o += 2048
        rstd = A32(o, 512); o += 2048
        assert o <= ARENA * 4, o
        HK = [("hT", 0), ("hT", 1), ("hT", 2)]
        s_ca = wload(cf_pieces(l, 0), key=("cf", l, 0))
        s_cb = wload(cf_pieces(l, 1), key=("cf", l, 1))
        wca = rview(s_ca, 0, 16, 512)
        wcb = rview(s_cb, 0, 16, 512)
        wc = f"cfw_{l}"
        for c in range(4):
            z = zb[c % 2]
            kz = ("zb", c % 2)
            p.op("pool", lambda e, z=z: e.memset(z, 0.0), w=[kz])
            for tb in range(3):
                b1 = psbank(0, 4)
                proj(b1, wcb, c * 128, 128, hT, tb * 512, 512, list(range(16)), [("ring", s_cb), HK[tb]])
                g = sg[tb % 2]
                p.op("act", lambda e, b1=b1, g=g: e.activation(out=g, in_=ps[:, b1, :], func=AF.Sigmoid), r=[("ps", b1)], w=[("sg", tb % 2)])
                b2 = psbank(0, 4)
                proj(b2, wca, c * 128, 128, hT, tb * 512, 512, list(range(16)), [("ring", s_ca), HK[tb]])
                if tb == 0:
                    for s_ in range(2):
                        z0 = cpos(s_ * 256)
                        p.op("dve", lambda e, b2=b2, g=g, s_=s_, z0=z0, z=z: e.tensor_tensor(
                            out=z[:, z0:z0 + 256], in0=ps[:, b2, s_ * 256:(s_ + 1) * 256], in1=g[:, s_ * 256:(s_ + 1) * 256], op=ALU.mult),
                            r=[("ps", b2), ("sg", tb % 2)], w=[kz])
                else:
                    z0 = cpos(tb * 512)
                    p.op("dve", lambda e, b2=b2, g=g, z0=z0, z=z: e.tensor_tensor(out=z[:, z0:z0 + 512], in0=ps[:, b2, :], in1=g, op=ALU.mult),
                         r=[("ps", b2), ("sg", tb % 2)], w=[kz])
            for j in range(31):
                p.op("dve", lambda e, j=j, c=c: e.tensor_scalar(out=dg[:, j, :], in0=identb[:], scalar1=C(wc, j * 4 + c), scalar2=None, op0=ALU.mult),
                     r=["identb", ("col", wc)], w=[("dg", j)])
            segs = [(0, 256), (256, 256), (512, 512), (1024, 512)]
            for (t0, n) in segs:
                bank = 4 + psbank(0, 4)
                p0 = cpos(t0) - 15
                fns = [lambda e, j=j, bank=bank, n=n, p0=p0, z=z: e.matmul(ps[:, bank, 0:n], lhsT=dg[:, j, :], rhs=z[:, p0 + j:p0 + j + n],
                                                                          start=(j == 0), stop=(j == 30)) for j in range(31)]
                p.mm(fns, r=[kz] + [("dg", j) for j in range(31)], w=[("ps", bank)])
                p.op("act", lambda e, bank=bank, n=n, t0=t0, c=c: e.activation(out=cy[:, c, t0:t0 + n], in_=ps[:, bank, 0:n], func=AF.Identity,
                                                                              bias=C(f"cfb_{l}", c)),
                     r=[("ps", bank), ("col", f"cfb_{l}")], w=[("cy", c)])
        for tb in range(3):
            ts_ = slice(tb * 512, (tb + 1) * 512)
            bm = psbank(0, 4)
            bq = psbank(0, 4)
            for c in range(4):
                p.mm([lambda e, c=c, bm=bm, ts_=ts_: e.matmul(ps[:, bm, :], lhsT=onesf[:], rhs=cy[:, c, ts_], start=(c == 0), stop=(c == 3))],
                     r=[("cy", c), "ones"], w=[("ps", bm)])
            for c in range(4):
                sq = sg[c % 2]
                p.op("act", lambda e, c=c, sq=sq, ts_=ts_: e.activation(out=sq, in_=cy[:, c, ts_], func=AF.Square), r=[("cy", c)], w=[("sg", c % 2)])
                p.mm([lambda e, c=c, sq=sq, bq=bq: e.matmul(ps[:, bq, :], lhsT=onesf[:], rhs=sq, start=(c == 0), stop=(c == 3))],
                     r=[("sg", c % 2), "ones"], w=[("ps", bq)])
            p.op("dve", lambda e, bm=bm: e.tensor_scalar(out=mean, in0=ps[:, bm, :], scalar1=1.0 / 512, scalar2=None, op0=ALU.mult),
                 r=[("ps", bm)], w=["mean"])
            p.op("dve", lambda e: e.tensor_tensor(out=rstd, in0=mean, in1=mean, op=ALU.mult), r=["mean"], w=["rstd"])
            p.op("dve", lambda e, bq=bq: e.scalar_tensor_tensor(out=rstd, in0=ps[:, bq, :], scalar=1.0 / 512, in1=rstd, op0=ALU.mult, op1=ALU.subtract),
                 r=[("ps", bq), "rstd"], w=["rstd"])
            p.op("act", lambda e: e.activation(out=rstd, in_=rstd, func=AF.Sqrt, bias=C("eps")), r=["rstd", ("col", "eps")], w=["rstd"])
            p.op("dve", lambda e: e.reciprocal(out=rstd, in_=rstd), r=["rstd"], w=["rstd"])
            for c in range(4):
                tmp = sg[c % 2]
                p.op("dve", lambda e, c=c, tmp=tmp, ts_=ts_: e.tensor_tensor(out=tmp, in0=cy[:, c, ts_], in1=mean, op=ALU.subtract),
                     r=[("cy", c), "mean"], w=[("sg", c % 2)])
                p.op("dve", lambda e, tmp=tmp: e.tensor_tensor(out=tmp, in0=tmp, in1=rstd, op=ALU.mult), r=[("sg", c % 2), "rstd"], w=[("sg", c % 2)])
                p.op("act", lambda e, c=c, tb=tb, tmp=tmp: e.activation(out=ycT[:, c, tb * 512:(tb + 1) * 512], in_=tmp, func=AF.Silu,
                                                                       scale=C(f"lng_{l}", c), bias=C(f"lnb_{l}", c)),
                     r=[("sg", c % 2), ("col", f"lng_{l}"), ("col", f"lnb_{l}")], w=[("ycT", c)] + [("dg", j) for j in range(31)])

    def xsweep(blocks, xs, banks):
        NB = len(xs)
        LA = NB - 2
        nb = len(blocks)

        def load(i):
            j, tb = blocks[i][0], blocks[i][1]
            p.dma("sp", [(xs[i % NB], xT_d[j, :, tb * 512:(tb + 1) * 512])], lsem(), r=[("xT", j, tb)], w=[("xblk", i % NB)])
        for i in range(min(LA, nb)):
            load(i)
        for i, (j, tb, gcol, pf, pre) in enumerate(blocks):
            if pre is not None:
                pre()
            bank = banks[i % len(banks)]
            pf(bank)
            if i + LA < nb:
                load(i + LA)
            xb_ = xs[i % NB]
            xk = ("xblk", i % NB)
            p.op("dve", lambda e, xb_=xb_, bank=bank, gcol=gcol: e.scalar_tensor_tensor(out=xb_, in0=ps[:, bank, :], scalar=gcol, in1=xb_,
                                                                                     op0=ALU.mult, op1=ALU.add),
                 r=[("ps", bank), xk], w=[xk])
            p.dma("sp", [(xT_d[j, :, tb * 512:(tb + 1) * 512], xb_)], ssem(), r=[xk], w=[("xT", j, tb)])

    def mg_pieces(l, j0, which):
        cbase = j0 * 128
        g0 = OFF_GATE + cbase
        if which == 0:
            return [(0, w_in[l][:, g0:g0 + 256], 16, 256), (4096, w_in[l][:, g0 + D:g0 + D + 256], 16, 256)]
        return [(0, w_in[l][:, g0 + 2 * D:g0 + 2 * D + 256], 16, 256),
                (4096, w_da_out[l][:, cbase:cbase + 256], 8, 256),
                (6144, w_sc_out[l][:, cbase:cbase + 256], 4, 256),
                (7168, w_cf_out[l][:, cbase:cbase + 256], 4, 256)]

    def phase_merge(l):
        attnT = A16(AT_OFF, 8 * NTOK).rearrange("p (h t) -> p h t", h=8)
        ybT = A16(YB_OFF, 4 * NTOK).rearrange("p (c t) -> p c t", c=4)
        ycT = A16(YC_OFF, 4 * NTOK).rearrange("p (c t) -> p c t", c=4)
        o = PH_OFF
        mT = A16(o, 8 * NTOK).rearrange("p (c t) -> p c t", c=8); o += 24576
        sa = [A32(o + i * 2048, 512) for i in range(2)]; o += 4096
        mm_ = A32(o, 512); o += 2048
        xs = [A32(o + i * 2048, 512) for i in range(4)]; o += 8192
        assert o <= ARENA * 4, o
        HK = [("hT", 0), ("hT", 1), ("hT", 2)]
        for half in range(2):
            for jq in range(4):
                j0 = half * 8 + jq * 2
                cbase = j0 * 128
                g0 = OFF_GATE + cbase
                sX = wload(mg_pieces(l, j0, 0), key=("mg", l, j0, 0))
                sY = wload(mg_pieces(l, j0, 1), key=("mg", l, j0, 1))
                gates = [rview(sX, 0, 16, 256), rview(sX, 4096, 16, 256), rview(sY, 0, 16, 256)]
                gsl = [sX, sX, sY]
                brs = [(rview(sY, 4096, 8, 256), attnT, 8, [("attnT", h) for h in range(8)]),
                       (rview(sY, 6144, 4, 256), ybT, 4, [("ybT", c) for c in range(4)]),
                       (rview(sY, 7168, 4, 256), ycT, 4, [("ycT", c) for c in range(4)])]
                for c in range(2):
                    j = j0 + c
                    for tb in range(3):
                        for g in range(3):
                            b1 = psbank(0, 4)
                            proj(b1, gates[g], c * 128, 128, hT, tb * 512, 512, list(range(16)), [("ring", gsl[g]), HK[tb]])
                            s_ = sa[g % 2]
                            p.op("act", lambda e, b1=b1, s_=s_, g=g, j=j: e.activation(out=s_, in_=ps[:, b1, :], func=AF.Sigmoid,
                                                                                      bias=C(f"bg_{l}", g * 16 + j)),
                                 r=[("ps", b1), ("col", f"bg_{l}")], w=[("sa", g % 2)])
                            b2 = psbank(0, 4)
                            wv, src, nkc, skeys = brs[g]
                            proj(b2, wv, c * 128, 128, src, tb * 512, 512, list(range(nkc)), [("ring", sY)] + skeys)
                            p.op("dve", lambda e, b2=b2, s_=s_: e.tensor_tensor(out=s_, in0=ps[:, b2, :], in1=s_, op=ALU.mult),
                                 r=[("ps", b2), ("sa", g % 2)], w=[("sa", g % 2)])
                            if g == 0:
                                p.op("dve", lambda e, s_=s_: e.tensor_copy(out=mm_, in_=s_), r=[("sa", 0)], w=["mm"])
                            elif g == 1:
                                p.op("dve", lambda e, s_=s_: e.tensor_tensor(out=mm_, in0=mm_, in1=s_, op=ALU.add), r=[("sa", 1), "mm"], w=["mm"])
                            else:
                                p.op("dve", lambda e, s_=s_, jq=jq, c=c, tb=tb: e.tensor_tensor(
                                    out=mT[:, jq * 2 + c, tb * 512:(tb + 1) * 512], in0=mm_, in1=s_, op=ALU.add),
                                    r=[("sa", 0), "mm"], w=[("mT", jq * 2 + c)])
            blocks = []
            st = {}
            for n in range(4):
                def pre(n=n, half=half):
                    si = wload([(0, w_out[l][half * 1024:(half + 1) * 1024, n * 512:(n + 1) * 512], 8, 512)])
                    st["si"] = si
                    st["wv"] = rview(si, 0, 8, 512)
                for c in range(4):
                    j = n * 4 + c
                    for tb in range(3):
                        v = 0 if tb == 0 else 1

                        def pf(bank, c=c, tb=tb):
                            proj(bank, st["wv"], c * 128, 128, mT, tb * 512, 512, list(range(8)), [("ring", st["si"])] + [("mT", k) for k in range(8)])
                        blocks.append((j, tb, C(f"mod_{l}_{v}", 32 + j), pf, pre if (c == 0 and tb == 0) else None))
            xsweep(blocks, xs, [4, 5, 6, 7])

    def ffn_pieces(l, cc, which):
        return [(0, w_ffn_in[l][:, which * FF + cc:which * FF + cc + 512], 16, 512)]

    def phase_ffn(l):
        o = 0
        gT = A16(o, 12 * NTOK).rearrange("p (c t) -> p c t", c=12); o += 36864
        su = [A32(o + i * 2048, 512) for i in range(2)]; o += 4096
        xs = [A32(o + i * 2048, 512) for i in range(6)]; o += 6 * 2048
        HK = [("hT", 0), ("hT", 1), ("hT", 2)]
        groups = [(0, 12), (12, 12), (24, 12), (36, 8)]
        for (c0, nch) in groups:
            for sl in range(nch // 4):
                cc = (c0 + sl * 4) * 128
                s_u = wload(ffn_pieces(l, cc, 0), key=("ffn", l, cc, 0))
                s_w = wload(ffn_pieces(l, cc, 1), key=("ffn", l, cc, 1))
                wu = rview(s_u, 0, 16, 512)
                ww = rview(s_w, 0, 16, 512)
                for c in range(4):
                    for tb in range(3):
                        b1 = psbank(0, 4)
                        proj(b1, wu, c * 128, 128, hT, tb * 512, 512, list(range(16)), [("ring", s_u), HK[tb]])
                        s_ = su[tb % 2]
                        p.op("act", lambda e, b1=b1, s_=s_: e.activation(out=s_, in_=ps[:, b1, :], func=AF.Silu), r=[("ps", b1)], w=[("su", tb % 2)])
                        b2 = psbank(0, 4)
                        proj(b2, ww, c * 128, 128, hT, tb * 512, 512, list(range(16)), [("ring", s_w), HK[tb]])
                        p.op("dve", lambda e, b2=b2, s_=s_, sl=sl, c=c, tb=tb: e.tensor_tensor(
                            out=gT[:, sl * 4 + c, tb * 512:(tb + 1) * 512], in0=ps[:, b2, :], in1=s_, op=ALU.mult),
                            r=[("ps", b2), ("su", tb % 2)], w=[("gT", sl * 4 + c)])
            blocks = []
            st = {}
            for n in range(4):
                def pre(n=n, c0=c0, nch=nch):
                    si = wload([(0, w_ffn_out[l][c0 * 128:(c0 + nch) * 128, n * 512:(n + 1) * 512], nch, 512)])
                    st["si"] = si
                    st["wv"] = rview(si, 0, nch, 512)
                for c in range(4):
                    j = n * 4 + c
                    for tb in range(3):
                        v = 0 if tb == 0 else 1

                        def pf(bank, c=c, tb=tb, nch=nch):
                            proj(bank, st["wv"], c * 128, 128, gT, tb * 512, 512, list(range(nch)), [("ring", st["si"])] + [("gT", k) for k in range(nch)])
                        blocks.append((j, tb, C(f"mod_{l}_{v}", 80 + j), pf, pre if (c == 0 and tb == 0) else None))
            xsweep(blocks, xs, [4, 5, 6, 7])

    def phase_final():
        for t in range(12):
            xb = A32((t % 2) * 8192, 2048).rearrange("p (j c) -> p j c", j=16)
            yb = A32(16384 + (t % 2) * 8192, 2048)
            kx = ("fx", t % 2)
            p.dma("sp", [(xb, xT_v[:, :, t * 128:(t + 1) * 128])], lsem(), w=[kx])
            bank = 0 if t % 2 == 0 else 5
            sq = A32(32768 + (t % 2) * 8192, 2048)
            ksq = ("fsq", t % 2)
            p.op("act", lambda e, sq=sq, xb=xb: e.activation(out=sq, in_=xb.rearrange("p j c -> p (j c)"), func=AF.Square), r=[kx], w=[ksq])
            p.mm([lambda e, j=j, sq=sq, bank=bank: e.matmul(ps[:, bank, 0:128], lhsT=onesf[:], rhs=sq[:, j * 128:(j + 1) * 128],
                                                           start=(j == 0), stop=(j == 15)) for j in range(16)],
                 r=[ksq, "ones"], w=[("ps", bank)])
            rs = A32(49152 + (t % 2) * 512, 128)
            krs = ("frs", t % 2)
            rstd_from_psum(bank, 128, rs, 1.0 / D, krs)
            for j in range(16):
                p.op("dve", lambda e, j=j, xb=xb, rs=rs: e.scalar_tensor_tensor(out=xb[:, j, :], in0=xb[:, j, :], scalar=C("gfin", j), in1=rs,
                                                                              op0=ALU.mult, op1=ALU.mult),
                     r=[kx, krs, ("col", "gfin")], w=[kx])
            for half in range(2):
                b0 = 1 + 2 * half + (t % 2) * 0
                fns = []
                for jj in range(8):
                    j = half * 8 + jj
                    fns.append(lambda e, j=j, jj=jj, b0=b0, xb=xb: e.transpose(
                        out=ps[:, b0 + jj // 4, (jj % 4) * 128:(jj % 4 + 1) * 128], in_=xb[:, j, :], identity=identf[:]))
                p.mm(fns, r=[kx, "identf"], w=[("ps", b0), ("ps", b0 + 1)])
                if half == 0:
                    p.op("act", lambda e, b0=b0, yb=yb: e.activation(out=yb[:, 0:1024].rearrange("p (a b) -> p a b", a=2), in_=ps[:, b0:b0 + 2, :], func=AF.Copy),
                         r=[("ps", b0), ("ps", b0 + 1)], w=[("fy", t % 2, 0)])
                else:
                    p.op("dve", lambda e, b0=b0, yb=yb: e.tensor_copy(out=yb[:, 1024:2048].rearrange("p (a b) -> p a b", a=2), in_=ps[:, b0:b0 + 2, :]),
                         r=[("ps", b0), ("ps", b0 + 1)], w=[("fy", t % 2, 1)])
            p.dma("sp", [(y_out[t * 128:(t + 1) * 128, :], yb)], ssem(), r=[("fy", t % 2, 0), ("fy", t % 2, 1)], w=[("yout", t)])

    plan = []
    for l in range(DEPTH):
        plan += [(f"n1_{l}", lambda l=l: phase_pre(l), lambda l=l: [prefetch(("mod", l, cb), mod_pieces(l, cb)) for cb in ((0, 1) if l == 0 else (12, 13))]),
                 (f"attn{l}", lambda l=l: phase_attn(l), lambda l=l: prefetch(("attn", l, 0), attn_pieces(l, 0))),
                 (f"sc{l}", lambda l=l: phase_sc(l), lambda l=l: [prefetch(("sc", l, i), sc_pieces(l, i)) for i in (1, 2)]),
                 (f"cf{l}", lambda l=l: phase_cf(l), lambda l=l: [prefetch(("cf", l, i), cf_pieces(l, i)) for i in (0, 1)]),
                 (f"merge{l}", lambda l=l: phase_merge(l), lambda l=l: [prefetch(("mg", l, 0, i), mg_pieces(l, 0, i)) for i in (0, 1)]),
                 (f"n2_{l}", lambda l=l: phase_n2(l), (lambda l=l: [prefetch(("mod", l + 1, cb), mod_pieces(l + 1, cb)) for cb in range(2)]) if l + 1 < DEPTH else None),
                 (f"ffn{l}", lambda l=l: phase_ffn(l), lambda l=l: [prefetch(("ffn", l, 0, i), ffn_pieces(l, 0, i)) for i in (0, 1)])]
    plan += [("final", phase_final, None)]
    p.barrier()
    hooked = set()
    for pi, (name, fn, hook) in enumerate(plan):
        fn()
        if not (upto is not None and name == upto):
            for pj in range(pi + 1, min(pi + 3, len(plan))):
                nh = plan[pj][2]
                if nh is not None:
                    if pj not in hooked:
                        hooked.add(pj)
                        nh()
                    break
        p.barrier()
        if upto is not None and name == upto:
            break
    if dbg:
        p.dma("sp", [(hT_dbg, hT[:].rearrange("p j t -> p (j t)"))], S_S[0])
        p.dma("sp", [(ar_dbg, arena[:])], S_S[1])
        p.dma("sp", [(cols_dbg, cols[:])], S_S[2])
    p.barrier(full=True)
    p.emit()
    p.stack.close()
    return nc


def _consts():
    ident = np.eye(128, dtype=np.float32)
    perm = np.zeros((128, 128), dtype=np.float32)
    sign = np.zeros(128, dtype=np.float32)
    for m in range(128):
        d = m % 64
        blk = d % 32
        if blk < 16:
            partner = m + 16
            sign[m] = -1.0
        else:
            partner = m - 16
            sign[m] = 1.0
        perm[partner, m] = 1.0
    n_freq = 16
    inv = (10000.0 ** (-np.arange(n_freq, dtype=np.float32) / n_freq)).astype(np.float32)
    tok = np.arange(1024)
    row_ids = (tok // 64).astype(np.float32)
    col_ids = (tok % 64).astype(np.float32)
    ang_row = row_ids[:, None] * inv[None, :]
    ang_col = col_ids[:, None] * inv[None, :]
    cos = np.zeros((128, 1024), dtype=np.float32)
    sin = np.zeros((128, 1024), dtype=np.float32)
    for m in range(128):
        d = m % 64
        ang = ang_row if d < 32 else ang_col
        i = d % 16
        cos[m] = np.cos(ang[:, i].astype(np.float32))
        sin[m] = np.sin(ang[:, i].astype(np.float32)) * sign[m]
    return ident, perm, cos.astype(np.float32), sin.astype(np.float32)


_NC_CACHE = {}


def kernel(x_prompt, x_sample, cache_k, cache_v, c, c_ctx, w_mod, b_mod, g_norm1, w_in,
           da_lambda, da_subln, w_da_out, sc_conv, w_sc_out, cf_conv, cf_conv_b, cf_ln_g,
           cf_ln_b, w_cf_out, b_gate, w_out, g_norm2, w_ffn_in, w_ffn_out, g_final):
    f = lambda a: np.ascontiguousarray(np.asarray(a, dtype=np.float32))
    x_prompt, x_sample, cache_k, cache_v, c, c_ctx = map(f, (x_prompt, x_sample, cache_k, cache_v, c, c_ctx))
    ident, perm, cos, sin = _consts()
    shared = dict(
        w_mod=f(w_mod), b_mod=f(b_mod), g_norm1=f(g_norm1), w_in=f(w_in), da_lambda=f(da_lambda).reshape(DEPTH, 256),
        da_subln=f(da_subln), w_da_out=f(w_da_out), sc_conv=f(sc_conv), w_sc_out=f(w_sc_out), cf_conv=f(cf_conv),
        cf_conv_b=f(cf_conv_b), cf_ln_g=f(cf_ln_g), cf_ln_b=f(cf_ln_b), w_cf_out=f(w_cf_out), b_gate=f(b_gate),
        w_out=f(w_out), g_norm2=f(g_norm2), w_ffn_in=f(w_ffn_in), w_ffn_out=f(w_ffn_out), g_final=f(g_final).reshape(1, D),
        c_ident=ident, c_perm=perm, c_cos=cos, c_sin=sin)
    in_maps = []
    for i in range(8):
        xin = np.concatenate([x_prompt[2 * i], x_prompt[2 * i + 1], x_sample[i]], axis=0)
        m = dict(shared)
        m["xin"] = np.ascontiguousarray(xin)
        m["ck"] = np.ascontiguousarray(cache_k[i].reshape(DEPTH, 256, 1024))
        m["cv"] = np.ascontiguousarray(cache_v[i].reshape(DEPTH, 256, 1024))
        m["cvec"] = np.ascontiguousarray(np.stack([c_ctx, c[i]], axis=0))
        in_maps.append(m)
    nc = bass.Bass("TRN2", target_bir_lowering=False)
    build(nc)
    res = run_bass_kernel_spmd(nc, in_maps, core_ids=list(range(8)))
    y_prompt = np.zeros((16, 256, D), np.float32)
    y_sample = np.zeros((8, 1024, D), np.float32)
    new_k = np.zeros((16, DEPTH, 256, 8, 128), np.float32)
    new_v = np.zeros((16, DEPTH, 256, 8, 128), np.float32)
    for i in range(8):
        r = res.results[i]
        y = r["y"]
        y_prompt[2 * i] = y[0:256]
        y_prompt[2 * i + 1] = y[256:512]
        y_sample[i] = y[512:1536]
        new_k[2 * i:2 * i + 2] = r["nk"].reshape(2, DEPTH, 256, 8, 128)
        new_v[2 * i:2 * i + 2] = r["nv"].reshape(2, DEPTH, 256, 8, 128)
    return (y_prompt, y_sample, new_k, new_v)
```
